# Optimizing a Trainium2 kernel written in Bass

```python
import math
import jax, jax.numpy as jnp
from jax import lax
import numpy as np

D_MODEL = 1024
BATCH = 8
SEQ = 2048
DEPTH = 2

N_MIXERS = 2
HEAD_DIM = 64
N_HEADS = D_MODEL // HEAD_DIM
DILATED_GROUPS = ((128, 1), (512, 4), (2048, 16))
N_GROUPS = len(DILATED_GROUPS)
QKV_WIDTH = N_GROUPS * 3 * N_HEADS * HEAD_DIM
N_BUCKETS = 32
MAX_DISTANCE = 1024
CONV_CHANNELS = D_MODEL
CONV_WIDTH = 31
D_FF = 4 * D_MODEL
RMS_EPS = 1e-6
LN_EPS = 1e-5
NEG_INF = -1e30
N_ATTN = (DEPTH + N_MIXERS - 1) // N_MIXERS
N_CONV = DEPTH // N_MIXERS

kernel_name = "hybrid_dilated_attn_conformer_conv_encoder"


def rmsnorm(x, g):
    xf = x.astype(jnp.float32)
    y = xf * lax.rsqrt(jnp.mean(xf * xf, axis=-1, keepdims=True) + RMS_EPS)
    return (y * g.astype(jnp.float32)).astype(x.dtype)


def layernorm(x, g, b):
    xf = x.astype(jnp.float32)
    mu = jnp.mean(xf, axis=-1, keepdims=True)
    var = jnp.mean(jnp.square(xf - mu), axis=-1, keepdims=True)
    y = (xf - mu) * lax.rsqrt(var + LN_EPS)
    return (y * g.astype(jnp.float32) + b.astype(jnp.float32)).astype(x.dtype)


def t5_bucket(rel):
    nb = N_BUCKETS // 2
    max_exact = nb // 2
    base = jnp.where(rel > 0, nb, 0)
    n = jnp.abs(rel)
    nf = jnp.maximum(n, 1).astype(jnp.float32)
    large = max_exact + (jnp.log(nf / max_exact) / math.log(MAX_DISTANCE / max_exact)
                         * (nb - max_exact)).astype(jnp.int32)
    large = jnp.minimum(large, nb - 1)
    return base + jnp.where(n < max_exact, n, large)


def dilated_window_attention(q, k, v, bias_table, window, dilation):
    B, S, H, Dh = q.shape
    r = dilation
    n_side = (window // 2) // r
    blk = n_side
    L = S // r
    nb = -(-L // blk)
    Lp = nb * blk

    def to_sub(t):
        t = t.reshape(B, L, r, H, Dh).transpose(0, 2, 1, 3, 4)
        return jnp.pad(t, ((0, 0), (0, 0), (0, Lp - L), (0, 0), (0, 0)))

    def band(t):
        t = jnp.pad(t, ((0, 0), (0, 0), (blk, blk), (0, 0), (0, 0))).reshape(B, r, nb + 2, blk, H, Dh)
        return jnp.concatenate([t[:, :, :-2], t[:, :, 1:-1], t[:, :, 2:]], axis=3)

    qb = to_sub(q).reshape(B, r, nb, blk, H, Dh)
    kb = band(to_sub(k))
    vb = band(to_sub(v))

    logits = jnp.einsum('brnqhd,brnkhd->brnhqk', qb, kb).astype(jnp.float32) * (Dh ** -0.5)
    qi = jnp.arange(blk)[:, None]
    kj = jnp.arange(3 * blk)[None, :] - blk
    delta = kj - qi
    bias = bias_table[t5_bucket(delta * r)].astype(jnp.float32).transpose(2, 0, 1)
    kpos = jnp.arange(nb)[:, None, None] * blk + kj[None]
    valid = (jnp.abs(delta) <= n_side)[None] & (kpos >= 0) & (kpos < L)
    logits = jnp.where(valid[:, None], logits + bias, NEG_INF)

    lse = jax.nn.logsumexp(logits, axis=-1)
    p = jnp.exp(logits - lse[..., None]).astype(v.dtype)
    out = jnp.einsum('brnhqk,brnkhd->brnqhd', p, vb)

    out = out.reshape(B, r, Lp, H, Dh)[:, :, :L].transpose(0, 2, 1, 3, 4).reshape(B, S, H, Dh)
    lse = lse.transpose(0, 1, 2, 4, 3).reshape(B, r, Lp, H)[:, :, :L]
    lse = lse.transpose(0, 2, 1, 3).reshape(B, S, H)
    return out, lse


def dilated_attention_mixer(h, w_qkv, w_o, rel_bias):
    B, S, _ = h.shape
    qkv = (h @ w_qkv).reshape(B, S, N_GROUPS, 3, N_HEADS, HEAD_DIM)
    outs, lses = [], []
    for g, (window, dilation) in enumerate(DILATED_GROUPS):
        o, lse = dilated_window_attention(qkv[:, :, g, 0], qkv[:, :, g, 1], qkv[:, :, g, 2],
                                          rel_bias[:, g * N_HEADS:(g + 1) * N_HEADS],
                                          window, dilation)
        outs.append(o)
        lses.append(lse)
    alpha = jax.nn.softmax(jnp.stack(lses), axis=0)
    o = jnp.einsum('gbsh,gbshd->bshd', alpha, jnp.stack(outs).astype(jnp.float32))
    return o.reshape(B, S, N_HEADS * HEAD_DIM).astype(h.dtype) @ w_o


def conformer_conv_mixer(h, w_pw1, b_pw1, w_dw, b_dw, ln_g, ln_b, w_pw2, b_pw2):
    u = h @ w_pw1 + b_pw1
    a, gate = jnp.split(u, 2, axis=-1)
    u = a * jax.nn.sigmoid(gate)
    u = lax.conv_general_dilated(u, w_dw[:, None, :], window_strides=(1,),
                                 padding=[(CONV_WIDTH // 2, CONV_WIDTH // 2)],
                                 dimension_numbers=('NWC', 'WIO', 'NWC'),
                                 feature_group_count=CONV_CHANNELS) + b_dw
    u = jax.nn.silu(layernorm(u, ln_g, ln_b))
    return u @ w_pw2 + b_pw2


def squared_relu_mlp(h, w_up, w_down):
    return jnp.square(jax.nn.relu(h @ w_up)) @ w_down


def setup_inputs(seed: int = 0) -> dict:
    key = jax.random.key(seed)
    ks = jax.random.split(key, 20)
    nrm = jax.random.normal
    f32 = jnp.float32

    def gain(k, shape):
        return 1.0 + 0.05 * nrm(k, shape, f32)

    return {
        "x": nrm(ks[0], (BATCH, SEQ, D_MODEL), f32),
        "rel_bias": 0.2 * nrm(ks[1], (N_BUCKETS, N_GROUPS * N_HEADS), f32),
        "norm_mix_pre": gain(ks[2], (DEPTH, D_MODEL)),
        "norm_mix_post": gain(ks[3], (DEPTH, D_MODEL)),
        "norm_mlp_pre": gain(ks[4], (DEPTH, D_MODEL)),
        "norm_mlp_post": gain(ks[5], (DEPTH, D_MODEL)),
        "attn_w_qkv": nrm(ks[6], (N_ATTN, D_MODEL, QKV_WIDTH), f32) * D_MODEL ** -0.5,
        "attn_w_o": nrm(ks[7], (N_ATTN, N_HEADS * HEAD_DIM, D_MODEL), f32) * (N_HEADS * HEAD_DIM) ** -0.5,
        "conv_w_pw1": nrm(ks[8], (N_CONV, D_MODEL, 2 * CONV_CHANNELS), f32) * D_MODEL ** -0.5,
        "conv_b_pw1": 0.02 * nrm(ks[9], (N_CONV, 2 * CONV_CHANNELS), f32),
        "conv_w_dw": nrm(ks[10], (N_CONV, CONV_WIDTH, CONV_CHANNELS), f32) * CONV_WIDTH ** -0.5,
        "conv_b_dw": 0.02 * nrm(ks[11], (N_CONV, CONV_CHANNELS), f32),
        "conv_ln_g": gain(ks[12], (N_CONV, CONV_CHANNELS)),
        "conv_ln_b": 0.02 * nrm(ks[13], (N_CONV, CONV_CHANNELS), f32),
        "conv_w_pw2": nrm(ks[14], (N_CONV, CONV_CHANNELS, D_MODEL), f32) * CONV_CHANNELS ** -0.5,
        "conv_b_pw2": 0.02 * nrm(ks[15], (N_CONV, D_MODEL), f32),
        "mlp_w_up": nrm(ks[16], (DEPTH, D_MODEL, D_FF), f32) * D_MODEL ** -0.5,
        "mlp_w_down": nrm(ks[17], (DEPTH, D_FF, D_MODEL), f32) * D_FF ** -0.5,
    }


def reference(x, rel_bias, norm_mix_pre, norm_mix_post, norm_mlp_pre, norm_mlp_post,
              attn_w_qkv, attn_w_o, conv_w_pw1, conv_b_pw1, conv_w_dw, conv_b_dw,
              conv_ln_g, conv_ln_b, conv_w_pw2, conv_b_pw2, mlp_w_up, mlp_w_down):
    for i in range(DEPTH):
        j = i // N_MIXERS
        h = rmsnorm(x, norm_mix_pre[i])
        if i % N_MIXERS == 0:
            m = dilated_attention_mixer(h, attn_w_qkv[j], attn_w_o[j], rel_bias)
        else:
            m = conformer_conv_mixer(h, conv_w_pw1[j], conv_b_pw1[j], conv_w_dw[j], conv_b_dw[j],
                                     conv_ln_g[j], conv_ln_b[j], conv_w_pw2[j], conv_b_pw2[j])
        x = x + rmsnorm(m, norm_mix_post[i])
        h = rmsnorm(x, norm_mlp_pre[i])
        x = x + rmsnorm(squared_relu_mlp(h, mlp_w_up[i], mlp_w_down[i]), norm_mlp_post[i])
    return x
```

```python
import numpy as np
import ml_dtypes
import concourse.bass as bass
import concourse.mybir as mybir
from concourse.bass_utils import run_bass_kernel_spmd

F32 = mybir.dt.float32
BF16 = mybir.dt.bfloat16
AF = mybir.ActivationFunctionType
ALU = mybir.AluOpType

D = 1024
S_LEN = 2048
NT = 16
DFF = 4096
RMS_EPS = 1e-6
LN_EPS = 1e-5


class Ctr:
    def __init__(self, sem, step):
        self.sem = sem
        self.step = step
        self.ops = []


class Op:
    __slots__ = ("eng", "fn", "deps", "marked", "ctr", "val", "is_dma")

    def __init__(self, eng, fn, ctr, is_dma):
        self.eng = eng
        self.fn = fn
        self.deps = []
        self.marked = is_dma
        self.ctr = ctr
        self.val = None
        self.is_dma = is_dma


class Buf:
    __slots__ = ("writer", "readers", "name")

    def __init__(self, name=""):
        self.writer = None
        self.readers = {}
        self.name = name


class Sched:
    ENGS = ("pe", "act", "dve", "pool", "sp")

    def __init__(self, nc, new_sem):
        self.nc = nc
        self.new_sem = new_sem
        self.q = {e: [] for e in self.ENGS}
        self.ectr = {e: Ctr(new_sem("c_" + e), 1) for e in ("pe", "act", "dve", "pool")}
        self.nops = 0

    def dma_ctr(self, name):
        return Ctr(self.new_sem(name), 16)

    def op(self, eng, fn, reads=(), writes=(), dma=None):
        is_dma = dma is not None
        o = Op(eng, fn, dma if is_dma else self.ectr.get(eng), is_dma)
        deps = []
        for b in reads:
            if b.writer is not None:
                deps.append((b.writer, True))
        for b in writes:
            if b.writer is not None:
                deps.append((b.writer, False))
            for r in b.readers.values():
                deps.append((r, False))
        key = ("dma", id(dma)) if is_dma else eng
        for (d, raw) in deps:
            if d is o:
                continue
            if (not d.is_dma) and (not is_dma) and d.eng == eng:
                if (not raw) or eng == "pe":
                    continue
            d.marked = True
            o.deps.append(d)
        for b in reads:
            b.readers[key] = o
        for b in writes:
            b.writer = o
            b.readers = {}
        self.q[eng].append(o)
        self.nops += 1
        return o

    def wait_all(self, eng, ops):
        o = Op(eng, None, None, False)
        for d in ops:
            d.marked = True
            o.deps.append(d)
        self.q[eng].append(o)
        return o

    def barrier(self):
        lasts = []
        for e in ("pe", "act", "dve", "pool"):
            for o in reversed(self.q[e]):
                if o.fn is not None and not o.is_dma:
                    lasts.append(o)
                    break
        seen = set()
        for e in self.ENGS:
            for o in reversed(self.q[e]):
                if o.is_dma and id(o.ctr) not in seen:
                    seen.add(id(o.ctr))
                    lasts.append(o)
        for e in self.ENGS:
            self.wait_all(e, lasts)

    def finalize(self):
        for e in self.ENGS:
            for o in self.q[e]:
                if o.ctr is not None:
                    o.ctr.ops.append(o)
        seen = set()
        for e in self.ENGS:
            for o in self.q[e]:
                c = o.ctr
                if c is None or id(c) in seen:
                    continue
                seen.add(id(c))
                n = 0
                for p in c.ops:
                    if p.marked:
                        n += c.step
                    p.val = n

    def emit_engine(self, name, e):
        waited = {}
        for o in self.q[name]:
            need = {}
            for d in o.deps:
                k = id(d.ctr)
                if d.val > need.get(k, (None, 0))[1]:
                    need[k] = (d.ctr, d.val)
            for k, (c, v) in need.items():
                if waited.get(k, 0) >= v:
                    continue
                e.wait_ge(c.sem, v)
                waited[k] = v
            if o.fn is None:
                continue
            ins = o.fn(e)
            if o.marked:
                ins.then_inc(o.ctr.sem, o.ctr.step)

    def emit(self):
        self.finalize()
        with self.nc.Block() as block:
            @block.tensor
            def _(e):
                self.emit_engine("pe", e)

            @block.scalar
            def _(e):
                self.emit_engine("act", e)

            @block.vector
            def _(e):
                self.emit_engine("dve", e)

            @block.gpsimd
            def _(e):
                self.emit_engine("pool", e)

            @block.sync
            def _(e):
                self.emit_engine("sp", e)


class Rot:
    def __init__(self, items):
        self.items = list(items)
        self.i = 0

    def next(self):
        it = self.items[self.i % len(self.items)]
        self.i += 1
        return it


class Prog:
    def __init__(self, phases):
        self.phases = phases
        self.nc = bass.Bass("TRN2", target_bir_lowering=False)
        self.stack = []
        self.S = None

    def enter(self, cm):
        v = cm.__enter__()
        self.stack.append(cm)
        return v

    def sem(self, name):
        return self.enter(self.nc.semaphore(name))

    def sb(self, name, shape, dt):
        return self.enter(self.nc.sbuf_tensor(name, shape, dt))

    def close(self):
        while self.stack:
            self.stack.pop().__exit__(None, None, None)

    def dram_in(self, name, shape, dt=F32):
        return self.nc.dram_tensor(name, list(shape), dt, kind="ExternalInput").ap()

    def mm(self, out, lhsT, rhs, start, stop, reads, writes, **kw):
        return self.S.op("pe", lambda e: e.matmul(out, lhsT, rhs, start=start, stop=stop, **kw),
                         reads, writes)

    def tr(self, out, in_, ident, reads, writes):
        return self.S.op("pe", lambda e: e.transpose(out, in_, ident), reads, writes)

    def act(self, out, in_, func, reads, writes, **kw):
        return self.S.op("act", lambda e: e.activation(out, in_, func, **kw), reads, writes)

    def dma(self, eng, out, in_, ctr, reads, writes):
        return self.S.op(eng, lambda e: e.dma_start(out=out, in_=in_), reads, writes, dma=ctr)

    def v(self, eng, fn, reads, writes):
        return self.S.op(eng, fn, reads, writes)

    def build(self):
        nc = self.nc
        P = self
        self.x_d = P.dram_in("x", [S_LEN, D])
        self.y_d = nc.dram_tensor("y", [S_LEN, D], F32, kind="ExternalOutput").ap()
        self.g_mix_pre = P.dram_in("norm_mix_pre", [2, D])
        self.g_mix_post = P.dram_in("norm_mix_post", [2, D])
        self.g_mlp_pre = P.dram_in("norm_mlp_pre", [2, D])
        self.g_mlp_post = P.dram_in("norm_mlp_post", [2, D])
        self.w_up = P.dram_in("mlp_w_up", [2, D, DFF])
        self.w_dn = P.dram_in("mlp_w_down", [2, DFF, D])
        self.ident_d = P.dram_in("c_ident", [128, 128], BF16)
        if "attn" in self.phases:
            self.w_qkv = P.dram_in("attn_w_qkv", [1, D, 9216])
            self.w_o = P.dram_in("attn_w_o", [1, D, D])
            self.rel_bias = P.dram_in("rel_bias", [32, 48])
            self.j_d = P.dram_in("c_J", [128, 128], BF16)
            self.oh_d = P.dram_in("c_oh", [32, 3, 384])
            self.mrow_d = P.dram_in("c_mrow", [1, 384])
        if "conv" in self.phases:
            self.c_w_pw1 = P.dram_in("conv_w_pw1", [1, D, 2 * D])
            self.c_w_pw2 = P.dram_in("conv_w_pw2", [1, D, D])
            self.c_b_pw2 = P.dram_in("conv_b_pw2", [1, D])
            self.c_ln_g = P.dram_in("l_conv_ln_g", [128, 8])
            self.c_ln_b = P.dram_in("l_conv_ln_b", [128, 8])
            self.c_b_dw = P.dram_in("l_conv_b_dw", [128, 8])
            self.c_b_pw1 = P.dram_in("l_conv_b_pw1", [128, 16])
            self.c_w_dw = P.dram_in("l_conv_w_dw", [8, 128, 31])

        self.S = Sched(nc, self.sem)
        S = self.S

        self.x_sb = P.sb("x_sb", [128, NT, D], F32)
        self.xb = [Buf("x%d" % t) for t in range(NT)]
        self.ident = P.sb("ident", [128, 128], BF16)
        self.identb = Buf("ident")
        self.gt = [P.sb("gain%d" % k, [128, D], F32) for k in range(2)]
        self.gtb = [Buf("g0"), Buf("g1")]
        self.gctr = [S.dma_ctr("gc0"), S.dma_ctr("gc1")]
        self.eps_t = P.sb("eps_t", [128, 4], F32)
        self.eps_ap = self.eps_t[:, 0:1]
        self.lneps_ap = self.eps_t[:, 1:2]
        epsb = Buf("eps")
        P.v("pool", lambda e: e.memset(self.eps_t[:, 0:1], RMS_EPS), [], [epsb])
        P.v("pool", lambda e: e.memset(self.eps_t[:, 1:2], LN_EPS), [], [epsb])
        P.v("pool", lambda e: e.memset(self.eps_t[:, 2:3], -0.5), [], [epsb])
        self.epsb = epsb
        self.junk = P.sb("junk", [128, D], BF16)
        self.junkb = Buf("junk")
        self.ss = [P.sb("ss%d" % k, [128, 4], F32) for k in range(4)]
        self.ssb = [Buf("ss%d" % k) for k in range(4)]
        self.ssrot = 0
        self.htok = [P.sb("htok%d" % k, [128, D], BF16) for k in range(2)]
        self.htokb = [Buf("ht0"), Buf("ht1")]
        self.htrot = 0
        self.tmp = [P.sb("tmp%d" % k, [128, D], F32) for k in range(1)]
        self.tmpb = [Buf("tmp0")]
        self.tmprot = 0
        self.psum = self.enter(nc.psum_tensor("psum", [128, 8, 512], F32))
        self.ps = [self.psum[:, k, :] for k in range(8)]
        self.psb = [Buf("ps%d" % k) for k in range(8)]

        c0 = S.dma_ctr("cident")
        P.dma("sp", self.ident[:, :], self.ident_d[:, :], c0, [], [self.identb])
        self.xctr = [S.dma_ctr("xc%d" % t) for t in range(NT)]
        self.pending_x = []
        for t in range(NT):
            f = (lambda t: (lambda: P.dma("sp", self.x_sb[:, t, :], self.x_d[t * 128:(t + 1) * 128, :], self.xctr[t], [], [self.xb[t]])))(t)
            if t < 4:
                f()
            else:
                self.pending_x.append(f)

        self.wub = nc.dram_tensor("wub_scratch", [2, D, DFF], BF16).ap()
        self.cast_q = []

        self.store_ops = []
        self.store_now = False
        self.resid_eng = "pool"
        for pi, ph in enumerate(self.phases):
            self.store_now = (pi == len(self.phases) - 1)
            mark = len(self.stack)
            if ph == "mlp0":
                self.mlp(0)
            elif ph == "mlp1":
                self.mlp(1)
            elif ph == "attn":
                self.attn()
            elif ph == "conv":
                self.conv()
            else:
                raise ValueError(ph)
            S.barrier()
            while len(self.stack) > mark:
                self.stack.pop().__exit__(None, None, None)

        assert len(self.store_ops) == NT
        S.wait_all("sp", self.store_ops)
        S.emit()
        self.close()
        return nc

    def pump(self, n):
        for _ in range(n):
            if self.cast_q:
                self.cast_q.pop(0)()

    def load_gain(self, k, g_dram_row):
        P = self
        P.dma("sp", self.gt[k][:, :], g_dram_row.partition_broadcast(128), self.gctr[k], [], [self.gtb[k]])

    def rstd_from_ss(self, ss_ap, ssbuf):
        P = self
        r = ss_ap[:, 3:4]
        q = ss_ap[:, 2:3]
        P.v("pool", lambda e: e.tensor_scalar(q, ss_ap[:, 0:1], 1.0 / D, RMS_EPS, ALU.mult, ALU.add), [ssbuf], [ssbuf])
        P.v("pool", lambda e: e.tensor_tensor(r, q, self.eps_t[:, 2:3], ALU.pow), [ssbuf, self.epsb], [ssbuf])
        return r

    def norm_A(self, t):
        P = self
        x_t = self.x_sb[:, t, :]
        k = self.ssrot % 4
        self.ssrot += 1
        ss, ssb = self.ss[k], self.ssb[k]
        P.act(self.junk[:, :], x_t, AF.Square, [self.xb[t]], [self.junkb, ssb], accum_out=ss[:, 0:1])
        r = self.rstd_from_ss(ss, ssb)
        hk = self.htrot % 2
        self.htrot += 1
        h, hb = self.htok[hk], self.htokb[hk]
        g = self.gt[0]
        P.v("dve", lambda e: e.scalar_tensor_tensor(h[:, :], x_t, r, g[:, :], ALU.mult, ALU.mult),
            [self.xb[t], ssb, self.gtb[0]], [hb])
        return h, hb

    def norm_B(self, hhb, hT_ap, hT_buf, tr_bank, use_act=True):
        P = self
        h, hb = hhb
        bank = self.ps[tr_bank]
        bb = self.psb[tr_bank]
        pv = bank.bitcast(BF16)
        for kc in range(8):
            P.tr(pv[:, kc * 128:(kc + 1) * 128], h[:, kc * 128:(kc + 1) * 128], self.ident[:, :],
                 [hb, self.identb], [bb])
        if use_act:
            P.act(hT_ap, pv.rearrange("p (k c) -> p k c", k=8), AF.Copy, [bb], [hT_buf])
        else:
            P.v("dve", lambda e: e.tensor_copy(hT_ap, pv.rearrange("p (k c) -> p k c", k=8)), [bb], [hT_buf])

    def norm_pre_many(self, tiles, dst_fn, tr_banks=(0,)):
        tiles = list(tiles)
        while self.pending_x:
            self.pending_x.pop(0)()
        prev = None
        for n, t in enumerate(tiles):
            cur = (self.norm_A(t), t, n)
            if prev is not None:
                ap, bf = dst_fn(prev[1])
                self.norm_B(prev[0], ap, bf, tr_banks[prev[2] % len(tr_banks)], prev[2] % 2 == 0)
            prev = cur
        ap, bf = dst_fn(prev[1])
        self.norm_B(prev[0], ap, bf, tr_banks[prev[2] % len(tr_banks)], prev[2] % 2 == 0)

    def post_norm_residual(self, t, bankA=None, m=None, mb=None):
        P = self
        k = self.ssrot % 4
        self.ssrot += 1
        ss, ssb = self.ss[k], self.ssb[k]
        if m is None:
            m = self.psum[:, bankA:bankA + 2, :]
            mb = [self.psb[bankA], self.psb[bankA + 1]]
        P.act(self.junk[:, :].rearrange("p (a b) -> p a b", a=2), m, AF.Square, mb, [self.junkb, ssb],
              accum_out=ss[:, 0:1])
        r = self.rstd_from_ss(ss, ssb)
        tmp, tb = self.tmp[0], self.tmpb[0]
        g = self.gt[1]
        P.v("dve", lambda e: e.scalar_tensor_tensor(tmp[:, :].rearrange("p (a b) -> p a b", a=2), m, r,
                                                    g[:, :].rearrange("p (a b) -> p a b", a=2), ALU.mult, ALU.mult),
            mb + [ssb, self.gtb[1]], [tb])
        x_t = self.x_sb[:, t, :]
        P.v(self.resid_eng, lambda e: e.tensor_tensor(x_t, x_t, tmp[:, :], ALU.add), [tb, self.xb[t]], [self.xb[t]])
        if self.store_now:
            self.store_ops.append(P.dma("sp", self.y_d[t * 128:(t + 1) * 128, :], x_t, self.xctr[t], [self.xb[t]], []))

    def mlp(self, i):
        P = self
        S = self.S
        nc = self.nc
        TB = 512
        NB = S_LEN // TB
        wdn = P.sb("wdn%d" % i, [128, 32, D], BF16)
        wdnb = [Buf("wdn%d" % s) for s in range(8)]
        wdnc = [S.dma_ctr("wdnc%d_%d" % (i, s)) for s in range(8)]
        uT = P.sb("uT%d" % i, [128, 32, TB], BF16)
        uTb = [Buf("uT%d" % j) for j in range(32)]
        hT = [P.sb("hTm%d_%d" % (i, k), [128, 8, TB], BF16) for k in range(1)]
        hTb = [[Buf("hTm%d_%d" % (k, q)) for q in range(4)] for k in range(1)]
        NSLOT = 3
        wup = [P.sb("wup%d_%d" % (i, k), [128, 8, 256], BF16) for k in range(NSLOT)]
        wupb = [Buf("wup%d" % k) for k in range(NSLOT)]
        wupc = [S.dma_ctr("wupc%d_%d" % (i, k)) for k in range(NSLOT)]
        rt = [P.sb("rt%d_%d" % (i, k), [128, 512], BF16) for k in range(2)]
        rtb = [Buf("rt0"), Buf("rt1")]

        self.load_gain(0, self.g_mlp_pre[i:i + 1, :])
        self.load_gain(1, self.g_mlp_post[i:i + 1, :])
        wdn_src = self.w_dn[i].rearrange("(j p) n -> p j n", p=128)
        scrb = [Buf("scr%d" % jg) for jg in range(16)]
        scrc = [S.dma_ctr("scrc%d_%d" % (i, jg)) for jg in range(16)]
        wup_f32 = self.w_up[i].rearrange("(k p) n -> p k n", p=128)
        wup_src = self.wub[i].rearrange("(k p) n -> p k n", p=128)

        up_banks = Rot([1, 2, 3])
        dn_banks = Rot([4, 6])
        slot_i = 0
        rt_i = 0
        for tb in range(NB):
            hk = 0
            self.norm_pre_many(range(tb * 4, tb * 4 + 4),
                               lambda t: (hT[hk][:, :, (t % 4) * 128:(t % 4 + 1) * 128], hTb[hk][t % 4]))
            for jg in range(16):
                sl = slot_i % NSLOT
                slot_i += 1
                cols = slice(jg * 256, (jg + 1) * 256)
                if tb == 0:
                    P.dma("pool", wup[sl][:, :, :], wup_f32[:, :, cols], wupc[sl], [], [wupb[sl]])
                    P.dma("sp", wup_src[:, :, cols], wup[sl][:, :, :], scrc[jg], [wupb[sl]], [scrb[jg]])
                    if jg % 2 == 1:
                        s8 = jg // 2
                        P.dma("pool", wdn[:, s8 * 4:(s8 + 1) * 4, :], wdn_src[:, s8 * 4:(s8 + 1) * 4, :], wdnc[s8], [], [wdnb[s8]])
                else:
                    P.dma("sp", wup[sl][:, :, :], wup_src[:, :, cols], wupc[sl], [scrb[jg]], [wupb[sl]])
                for j in range(2):
                    bk = up_banks.next()
                    for kc in range(8):
                        P.mm(self.ps[bk], wup[sl][:, kc, j * 128:(j + 1) * 128], hT[hk][:, kc, :],
                             kc == 0, kc == 7, [wupb[sl]] + hTb[hk], [self.psb[bk]])
                    rk = rt_i % 2
                    rt_i += 1
                    P.act(rt[rk][:, :], self.ps[bk], AF.Relu, [self.psb[bk]], [rtb[rk]])
                    jj = jg * 2 + j
                    P.v("dve", (lambda o, a: lambda e: e.tensor_tensor(o, a, a, ALU.mult))(uT[:, jj, :], rt[rk][:, :]),
                        [rtb[rk]], [uTb[jj]])
            for q in range(4):
                t = tb * 4 + q
                b0 = dn_banks.next()
                for nh in range(2):
                    bk = b0 + nh
                    for j in range(32):
                        P.mm(self.ps[bk], uT[:, j, q * 128:(q + 1) * 128], wdn[:, j, nh * 512:(nh + 1) * 512],
                             j == 0, j == 31, [uTb[j], wdnb[j // 4]], [self.psb[bk]])
                self.post_norm_residual(t, b0)


    def conv(self):
        P = self
        S = self.S
        nc = self.nc
        LI = 1
        PAD = 16
        hT = P.sb("c_hT", [128, 8, S_LEN], BF16)
        hTb = [Buf("c_hT%d" % t) for t in range(NT)]
        uT = P.sb("c_uT", [128, 8, S_LEN + 2 * PAD], BF16)
        uTb = [[Buf("c_uT%d_%d" % (c, g)) for g in range(4)] for c in range(8)]
        padb = [Buf("c_pad%d" % c) for c in range(8)]
        w1 = [P.sb("c_w1_%d" % k, [128, 8, 2, 128], BF16) for k in range(2)]
        w1b = [Buf("w1_0"), Buf("w1_1")]
        w1c = [S.dma_ctr("w1c0"), S.dma_ctr("w1c1")]
        w2 = P.sb("c_w2", [128, 8, D], BF16)
        w2b = Buf("w2")
        w2c = S.dma_ctr("w2c")
        diag = [P.sb("c_diag%d" % k, [128, 31, 128], BF16) for k in range(1)]
        diagb = [Buf("dg0")]
        sm = P.sb("c_small", [128, 40], F32)
        smb = Buf("c_small")
        smc = S.dma_ctr("smc")
        wdw = P.sb("c_wdw", [128, 8, 31], F32)
        wdwb = Buf("wdw")
        brow = P.sb("c_brow", [1, D], F32)
        bh = P.sb("c_bh", [1, D], BF16)
        bl = P.sb("c_bl", [1, D], BF16)
        ones_rb = P.sb("c_ones_rb", [1, 128], BF16)
        browb = Buf("brow")
        vsq = [P.sb("c_vsq%d" % k, [128, 512], BF16) for k in range(2)]
        vsqb = [Buf("vsq0"), Buf("vsq1")]
        yt = [P.sb("c_yt%d" % k, [128, 512], F32) for k in range(2)]
        ytb = [Buf("yt0"), Buf("yt1")]
        sig, sigb = yt, ytb
        row = [P.sb("c_row%d" % k, [1, 2, 512], F32) for k in range(2)]
        rowb = [Buf("row0"), Buf("row1")]
        ones_c = P.sb("c_ones_c", [128, 1], BF16)
        ones_r = P.sb("c_ones_r", [1, 128], F32)
        onesb = Buf("ones")

        self.load_gain(0, self.g_mix_pre[LI:LI + 1, :])
        self.load_gain(1, self.g_mix_post[LI:LI + 1, :])
        P.dma("sp", sm[:, 0:8], self.c_ln_g[:, :], smc, [], [smb])
        P.dma("sp", sm[:, 8:16], self.c_ln_b[:, :], smc, [], [smb])
        P.dma("sp", sm[:, 16:24], self.c_b_dw[:, :], smc, [], [smb])
        P.dma("sp", sm[:, 24:40], self.c_b_pw1[:, :], smc, [], [smb])
        P.dma("sp", wdw[:, :, :], self.c_w_dw.rearrange("c p j -> p c j"), smc, [], [wdwb])
        P.dma("sp", brow[:, :], self.c_b_pw2[0:1, :], smc, [], [browb])
        P.v("pool", lambda e: e.memset(ones_rb[:, :], 1.0), [], [onesb])
        P.v("dve", lambda e: e.tensor_copy(bh[:, :], brow[:, :]), [browb], [browb])
        P.v("dve", lambda e: e.tensor_tensor(brow[:, :], brow[:, :], bh[:, :], ALU.subtract), [browb], [browb])
        P.v("dve", lambda e: e.tensor_copy(bl[:, :], brow[:, :]), [browb], [browb])
        P.v("pool", lambda e: e.memset(ones_c[:, :], 1.0), [], [onesb])
        P.v("pool", lambda e: e.memset(ones_r[:, :], 1.0), [], [onesb])
        for c in range(8):
            P.v("pool", (lambda c: lambda e: e.memset(uT[:, c, 0:PAD], 0.0))(c), [], [padb[c]])
            P.v("pool", (lambda c: lambda e: e.memset(uT[:, c, PAD + S_LEN:], 0.0))(c), [], [padb[c]])
        w2_src = self.c_w_pw2[0].rearrange("(k p) n -> p k n", p=128)

        w1_src = self.c_w_pw1[0].rearrange("(k p) n -> p k n", p=128)
        a_banks = Rot([1, 2])
        g_banks = Rot([3, 4])
        si = 0
        for j in range(8):
            sl = j % 2
            P.dma("pool", w1[sl][:, :, 0, :], w1_src[:, :, j * 128:(j + 1) * 128], w1c[sl], [], [w1b[sl]])
            P.dma("pool", w1[sl][:, :, 1, :], w1_src[:, :, D + j * 128:D + (j + 1) * 128], w1c[sl], [], [w1b[sl]])
            if j == 1:
                P.dma("pool", w2[:, :, :], w2_src, w2c, [], [w2b])
            for tg in range(4):
                if j == 0:
                    self.norm_pre_many(range(tg * 4, tg * 4 + 4), lambda t: (hT[:, :, t * 128:(t + 1) * 128], hTb[t]), tr_banks=(0, 7))
                ba = a_banks.next()
                bg = g_banks.next()
                hb = hTb[tg * 4:(tg + 1) * 4]
                for kc in range(8):
                    P.mm(self.ps[bg], w1[sl][:, kc, 1, :], hT[:, kc, tg * 512:(tg + 1) * 512], kc == 0, kc == 7,
                         [w1b[sl]] + hb, [self.psb[bg]])
                for kc in range(8):
                    P.mm(self.ps[ba], w1[sl][:, kc, 0, :], hT[:, kc, tg * 512:(tg + 1) * 512], kc == 0, kc == 7,
                         [w1b[sl]] + hb, [self.psb[ba]])
                sk = si % 2
                si += 1
                P.act(sig[sk][:, :], self.ps[bg], AF.Sigmoid, [self.psb[bg], smb], [sigb[sk]],
                      bias=sm[:, 24 + 8 + j:24 + 8 + j + 1])
                o = uT[:, j, PAD + tg * 512:PAD + (tg + 1) * 512]
                P.v("dve", (lambda o, pa, bb, sg: lambda e: e.scalar_tensor_tensor(o, pa, bb, sg, ALU.add, ALU.mult))(
                    o, self.ps[ba], sm[:, 24 + j:24 + j + 1], sig[sk][:, :]),
                    [self.psb[ba], smb, sigb[sk]], [uTb[j][tg]])

        c_banks = Rot([5, 6, 7])
        for c in range(8):
            dk = 0
            for j in range(31):
                P.v("dve", (lambda o, w: lambda e: e.tensor_scalar(o, self.ident[:, :], w, None, ALU.mult))(
                    diag[dk][:, j, :], wdw[:, c, j:j + 1]), [self.identb, wdwb], [diagb[dk]])
            for tg in range(4):
                bk = c_banks.next()
                for j in range(31):
                    off = PAD + tg * 512 + j - 15
                    rb = [uTb[c][tg]]
                    if j < 15:
                        rb.append(uTb[c][tg - 1] if tg > 0 else padb[c])
                    if j > 15:
                        rb.append(uTb[c][tg + 1] if tg < 3 else padb[c])
                    P.mm(self.ps[bk], diag[dk][:, j, :], uT[:, c, off:off + 512], j == 0, j == 30,
                         [diagb[dk]] + rb, [self.psb[bk]])
                P.act(hT[:, c, tg * 512:(tg + 1) * 512], self.ps[bk], AF.Identity, [self.psb[bk], smb],
                      hTb[tg * 4:(tg + 1) * 4], bias=sm[:, 16 + c:16 + c + 1])

        BS1, BS2, BA, BB = 2, 3, 4, 5
        pw2_banks = Rot([6, 0])
        stc = {"vi": 0, "yi": 0}

        def ln_stats(tg):
            hb = hTb[tg * 4:(tg + 1) * 4]
            for c in range(8):
                vk = stc["vi"] % 2
                stc["vi"] += 1
                vv = hT[:, c, tg * 512:(tg + 1) * 512]
                P.act(vsq[vk][:, :], vv, AF.Square, hb, [vsqb[vk]])
                P.mm(self.ps[BS1][0:1, :], ones_c[:, :], vv, c == 0, c == 7, hb + [onesb], [self.psb[BS1]])
                P.mm(self.ps[BS2][0:1, :], ones_c[:, :], vsq[vk][:, :], c == 0, c == 7, [vsqb[vk], onesb], [self.psb[BS2]])

        def ln_rowmath(tg):
            k = tg % 2
            Ar, Br = row[k][:, 0, :], row[k][:, 1, :]
            rb = rowb[k]
            s1, s2 = self.ps[BS1][0:1, :], self.ps[BS2][0:1, :]
            P.v("dve", lambda e: e.tensor_scalar(Br, s1, 1.0 / D, None, ALU.mult), [self.psb[BS1]], [rb])
            P.v("dve", lambda e: e.tensor_tensor(Ar, Br, Br, ALU.mult), [rb], [rb])
            P.v("dve", lambda e: e.scalar_tensor_tensor(Ar, s2, 1.0 / D, Ar, ALU.mult, ALU.subtract),
                [self.psb[BS2], rb], [rb])
            P.act(Ar, Ar, AF.Sqrt, [rb, self.epsb], [rb], bias=self.lneps_ap[0:1, :])
            P.v("dve", lambda e: e.reciprocal(Ar, Ar), [rb], [rb])
            P.v("dve", lambda e: e.scalar_tensor_tensor(Br, Br, -1.0, Ar, ALU.mult, ALU.mult), [rb], [rb])

        def ln_bcast(tg):
            k = tg % 2
            P.mm(self.ps[BA], ones_r[:, :], row[k][:, 0, :], True, True, [rowb[k], onesb], [self.psb[BA]])
            P.mm(self.ps[BB], ones_r[:, :], row[k][:, 1, :], True, True, [rowb[k], onesb], [self.psb[BB]])

        def ln_apply(tg, cs):
            hb = hTb[tg * 4:(tg + 1) * 4]
            for c in cs:
                yk = stc["yi"] % 2
                stc["yi"] += 1
                vv = hT[:, c, tg * 512:(tg + 1) * 512]
                y = yt[yk][:, :]
                P.v("dve", (lambda y, vv: lambda e: e.tensor_tensor(y, vv, self.ps[BA], ALU.mult))(y, vv),
                    hb + [self.psb[BA]], [ytb[yk]])
                P.v("dve", (lambda y: lambda e: e.tensor_tensor(y, y, self.ps[BB], ALU.add))(y),
                    [ytb[yk], self.psb[BB]], [ytb[yk]])
                P.act(uT[:, c, PAD + tg * 512:PAD + (tg + 1) * 512], y, AF.Silu, [ytb[yk], smb], [uTb[c][tg]],
                      scale=sm[:, c:c + 1], bias=sm[:, 8 + c:8 + c + 1])

        def pw2_tile(tg, q):
            if True:
                t = tg * 4 + q
                b0 = pw2_banks.next()
                for nh in range(2):
                    for c in range(8):
                        P.mm(self.ps[b0 + nh], uT[:, c, PAD + t * 128:PAD + (t + 1) * 128], w2[:, c, nh * 512:(nh + 1) * 512],
                             c == 0, False, [uTb[c][tg], w2b], [self.psb[b0 + nh]])
                    P.mm(self.ps[b0 + nh], ones_rb[:, :], bh[:, nh * 512:(nh + 1) * 512], False, False,
                         [onesb, browb], [self.psb[b0 + nh]])
                    P.mm(self.ps[b0 + nh], ones_rb[:, :], bl[:, nh * 512:(nh + 1) * 512], False, True,
                         [onesb, browb], [self.psb[b0 + nh]])
                self.post_norm_residual(t, b0)

        self.resid_eng = "dve"
        ln_stats(0)
        ln_rowmath(0)
        ln_bcast(0)
        for tg in range(5):
            if tg + 1 < 4:
                ln_stats(tg + 1)
            for q in range(4):
                if tg < 4:
                    ln_apply(tg, (2 * q, 2 * q + 1))
                if tg >= 1:
                    pw2_tile(tg - 1, q)
            if tg + 1 < 4:
                ln_rowmath(tg + 1)
                ln_bcast(tg + 1)
        self.resid_eng = "pool"

    def attn(self):
        P = self
        S = self.S
        nc = self.nc
        LI = 0
        GROUPS = (1, 4, 16)
        hT = P.sb("a_hT", [128, 8, S_LEN], BF16)
        hTb = [Buf("a_hT%d" % t) for t in range(NT)]
        oT = P.sb("a_oT", [128, 8, S_LEN], BF16)
        oTb = [Buf("a_oT%d" % k) for k in range(8)]
        wsl = [P.sb("a_w%d" % k, [128, 8, 3, 128], BF16) for k in range(2)]
        wslb = [Buf("a_w0"), Buf("a_w1")]
        wslc = [S.dma_ctr("a_wc0"), S.dma_ctr("a_wc1")]
        qZ = P.sb("a_qZ", [128, 2, S_LEN], BF16)
        kT = P.sb("a_kT", [128, S_LEN], BF16)
        vT = P.sb("a_vT", [128, S_LEN], BF16)
        qZb, kTb, vTb = Buf("qZ"), Buf("kT"), Buf("vT")
        VZ = P.sb("a_VZ", [128, 16, 2, 128], BF16)
        VZb = Buf("VZ")
        OZ = P.sb("a_OZ", [128, 2, 128], BF16)
        OZb = Buf("OZ")
        accw = P.sb("a_acc", [128, 2 * S_LEN], F32)
        nacc = accw[:, 0:S_LEN]
        dacc = accw[:, S_LEN:2 * S_LEN]
        naccb, daccb = Buf("nacc"), Buf("dacc")
        PT = [P.sb("a_PT%d" % k, [128, 2, 256], BF16) for k in range(3)]
        PTb = [Buf("PT%d" % k) for k in range(3)]
        BM = [P.sb("a_BM%d" % k, [128, 2, 256], BF16) for k in range(2)]
        BMb = [Buf("BM0"), Buf("BM1")]
        BMc = [S.dma_ctr("bmc0"), S.dma_ctr("bmc1")]
        jmat = P.sb("a_J", [128, 128], BF16)
        jb = Buf("J")
        relb = P.sb("a_relb", [32, 48], F32)
        ones16 = P.sb("a_ones16", [1, 16], F32)
        neg1 = P.sb("a_neg1", [128, 1], F32)
        oh = accw[0:32, 0:1152].rearrange("p (g u) -> p g u", g=3)
        mrow = accw[0:1, 1152:1536]
        gsb = accw[0:16, 1536:2112].bitcast(BF16).rearrange("p (g u) -> p g u", g=3)
        cb = Buf("a_consts")
        gsbb = Buf("gsb")
        cc = S.dma_ctr("a_cc")
        gd_t = nc.dram_tensor("a_gscratch", [48, 384], BF16)
        gd = gd_t.ap()
        gdb = Buf("gd")
        gdc = S.dma_ctr("a_gdc")

        self.load_gain(0, self.g_mix_pre[LI:LI + 1, :])
        self.load_gain(1, self.g_mix_post[LI:LI + 1, :])
        P.dma("sp", jmat[:, :], self.j_d[:, :], cc, [], [jb])
        P.dma("sp", relb[:, :], self.rel_bias[:, :], cc, [], [cb])
        P.dma("sp", oh, self.oh_d[:, :, :], cc, [], [cb])
        P.dma("sp", mrow, self.mrow_d[:, :], cc, [], [cb])
        P.v("pool", lambda e: e.memset(ones16[:, :], 1.0), [], [cb])
        P.v("pool", lambda e: e.memset(neg1[:, :], -1.0), [], [cb])
        P.v("pool", lambda e: e.memset(VZ[:, :, :, :], 0.0), [], [VZb])
        P.v("pool", lambda e: e.memset(qZ[:, :, :], 0.0), [], [qZb])
        P.v("pool", lambda e: e.memset(OZ[:, :, :], 0.0), [], [OZb])
        P.v("pool", lambda e: e.memset(OZ[:, 0, 0:64], 1.0), [OZb], [OZb])
        P.v("pool", lambda e: e.memset(OZ[:, 1, 64:128], 1.0), [OZb], [OZb])

        for g in range(3):
            bk = 1 + g
            P.mm(self.ps[bk][0:16, 0:384], relb[:, g * 16:(g + 1) * 16], oh[:, g, :], True, False, [cb], [self.psb[bk]])
            P.mm(self.ps[bk][0:16, 0:384], ones16[:, :], mrow, False, True, [cb], [self.psb[bk]])
            P.act(gsb[:, g, :], self.ps[bk][0:16, 0:384], AF.Copy, [self.psb[bk]], [gsbb])
        P.dma("sp", gd.rearrange("(g h) u -> h g u", g=3), gsb, gdc, [gsbb], [gdb])

        wq_src = self.w_qkv[0].rearrange("(k p) n -> p k n", p=128)
        qkv_banks = Rot([1, 2])
        st_banks = Rot([3, 4])
        nd_sets = Rot([(5, 6), (7, 0)])
        st = {"wi": 0, "bi": 0, "pti": 0}

        def stage_a(stp, r, L, tpc, bs):
            (c, j, lo, hi, coff, qlo, first, last, unit) = stp
            w = hi - lo
            woff = lo - (128 * j - 64)
            bst = st_banks.next()
            base = c * L
            o3 = self.ps[bst].rearrange("p (h x) -> p h x", h=2)[:, :, 0:w]
            P.mm(o3, kT[:, base + 128 * j:base + 128 * j + 128], qZ[:, :, base + lo:base + hi], True, False,
                 [kTb, qZb], [self.psb[bst]], skip_group_check=True)
            P.mm(o3, jmat[:, :], BM[bs][:, :, woff:woff + w], False, True, [jb, BMb[bs]], [self.psb[bst]],
                 skip_group_check=True)
            pk = st["pti"] % 3
            st["pti"] += 1
            P.act(PT[pk][:, :, 0:w], o3, AF.Exp, [self.psb[bst]], [PTb[pk]])
            return pk

        def stage_b(stp, pk, r, L, tpc, gi, ndset):
            (c, j, lo, hi, coff, qlo, first, last, unit) = stp
            bN, bD = ndset
            w = hi - lo
            ti = c * tpc + j
            c0 = coff + lo - qlo
            for h in range(2):
                P.mm(self.ps[bN][:, c0:c0 + w], VZ[:, ti, h, :], PT[pk][:, h, 0:w], first and h == 0, False,
                     [VZb, PTb[pk]], [self.psb[bN]], skip_group_check=True)
            for h in range(2):
                P.mm(self.ps[bD][:, c0:c0 + w], OZ[:, h, :], PT[pk][:, h, 0:w], first and h == 0, False,
                     [OZb, PTb[pk]], [self.psb[bD]], skip_group_check=True)
            if last:
                (c_first, uqlo, uqhi, _) = unit[0]
                ncl = len(unit)
                wq = uqhi - uqlo
                for (acc, accb, bk) in ((nacc, naccb, bN), (dacc, daccb, bD)):
                    view = acc.rearrange("p (l r) -> p r l", r=r)[:, c_first:c_first + ncl, uqlo:uqhi]
                    pin = self.ps[bk][:, 0:ncl * wq].rearrange("p (c l) -> p c l", c=ncl)
                    if gi == 0:
                        P.v("dve", (lambda o, i_: lambda e: e.tensor_copy(o, i_))(view, pin), [self.psb[bk]], [accb])
                    else:
                        P.v("dve", (lambda o, i_: lambda e: e.tensor_tensor(o, i_, o, ALU.add))(view, pin),
                            [self.psb[bk], accb], [accb])

        def load_w(it):
            hp_, gi_ = it // 3, it % 3
            sl_ = it % 2
            for s3 in range(3):
                col = (gi_ * 3 + s3) * 1024 + hp_ * 128
                P.dma("pool", wsl[sl_][:, :, s3, :], wq_src[:, :, col:col + 128], wslc[sl_], [], [wslb[sl_]])

        def load_bm(it):
            hp_, gi_ = it // 3, it % 3
            bs_ = it % 2
            src = bass.AP(gd_t, (gi_ * 16 + 2 * hp_) * 384, [[1, 128], [384, 2], [1, 256]])
            P.dma("sp", BM[bs_][:, :, :], src, BMc[bs_], [gdb], [BMb[bs_]])

        wo_ap, wo_buf = [], []

        def load_wo():
            wo_src = self.w_o[0].rearrange("(k p) n -> p k n", p=128)
            w0f = wsl[0][:, :, :, :].rearrange("p a b c -> p (a b c)")
            w1f = wsl[1][:, :, :, :].rearrange("p a b c -> p (a b c)")
            for k in range(8):
                if k < 3:
                    wo_ap.append(w0f[:, k * 1024:(k + 1) * 1024]); wo_buf.append(wslb[0])
                elif k < 6:
                    wo_ap.append(w1f[:, (k - 3) * 1024:(k - 2) * 1024]); wo_buf.append(wslb[1])
                else:
                    wo_ap.append(vT[:, (k - 6) * 1024:(k - 5) * 1024]); wo_buf.append(vTb)
            woc = [S.dma_ctr("a_woc%d" % k) for k in range(3)]
            for k in range(8):
                P.dma("pool", wo_ap[k], wo_src[:, k, :], woc[0 if k < 3 else (1 if k < 6 else 2)], [], [wo_buf[k]])

        pending_norm = []

        def emit_norm():
            while pending_norm:
                hp_ = pending_norm.pop(0)
                for ch in range(4):
                    cs = slice(ch * 512, (ch + 1) * 512)
                    P.v("dve", (lambda a: lambda e: e.reciprocal(a, a))(dacc[:, cs]), [daccb], [daccb])
                for ch in range(4):
                    cs = slice(ch * 512, (ch + 1) * 512)
                    P.v("pool", (lambda o, a, b: lambda e: e.tensor_tensor(o, a, b, ALU.mult))(oT[:, hp_, cs], nacc[:, cs], dacc[:, cs]),
                        [naccb, daccb], [oTb[hp_]])

        for hp in range(8):
            for gi, r in enumerate(GROUPS):
                L = S_LEN // r
                tpc = L // 128
                it = hp * 3 + gi
                sl = it % 2
                bs = it % 2
                if it == 0:
                    load_w(0)
                    load_w(1)
                    load_bm(0)
                if it + 1 < 24:
                    load_bm(it + 1)
                emit_norm()
                lpt = 512 // r
                order = [(s3, tg) for s3 in range(3) for tg in range(4)]
                if it == 0:
                    order = [(s3, tg) for tg in range(4) for s3 in range(3)]
                for (s3, tg) in order:
                    if it == 0 and s3 == 0:
                        self.norm_pre_many(range(tg * 4, tg * 4 + 4), lambda t: (hT[:, :, t * 128:(t + 1) * 128], hTb[t]), tr_banks=(0, 7))
                    if True:
                        bk = qkv_banks.next()
                        for kc in range(8):
                            P.mm(self.ps[bk], wsl[sl][:, kc, s3, :], hT[:, kc, tg * 512:(tg + 1) * 512], kc == 0, kc == 7,
                                 [wslb[sl]] + hTb[tg * 4:(tg + 1) * 4], [self.psb[bk]])
                        pin = self.ps[bk].rearrange("p (l c) -> p c l", c=r)
                        if s3 == 0:
                            for h in range(2):
                                rows = slice(64 * h, 64 * h + 64)
                                o = qZ[rows, h, :].rearrange("p (c l) -> p c l", c=r)[:, :, tg * lpt:(tg + 1) * lpt]
                                P.act(o, pin[rows], AF.Copy, [self.psb[bk]], [qZb], scale=0.125)
                        else:
                            dst, dstb = (kT, kTb) if s3 == 1 else (vT, vTb)
                            o = dst[:, :].rearrange("p (c l) -> p c l", c=r)[:, :, tg * lpt:(tg + 1) * lpt]
                            if gi == 0 and hp > 0:
                                P.act(o, pin, AF.Copy, [self.psb[bk]], [dstb])
                            else:
                                P.v("dve", (lambda o, pin: lambda e: e.tensor_copy(o, pin))(o, pin), [self.psb[bk]], [dstb])
                if it + 2 < 24:
                    load_w(it + 2)
                if it >= 1:
                    self.pump(2)
                for half in range(2):
                    bk = qkv_banks.next()
                    pv = self.ps[bk].bitcast(BF16)
                    for k8 in range(8):
                        ti = half * 8 + k8
                        P.tr(pv[:, k8 * 128:(k8 + 1) * 128], vT[:, ti * 128:(ti + 1) * 128], self.ident[:, :],
                             [vTb, self.identb], [self.psb[bk]])
                    pv3 = pv.rearrange("p (t c) -> p t c", t=8)
                    P.act(VZ[:, half * 8:(half + 1) * 8, 0, 0:64], pv3[:, :, 0:64], AF.Copy, [self.psb[bk]], [VZb])
                    P.v("dve", (lambda o, i_: lambda e: e.tensor_copy(o, i_))(VZ[:, half * 8:(half + 1) * 8, 1, 64:128], pv3[:, :, 64:128]),
                        [self.psb[bk]], [VZb])
                if it == 23:
                    load_wo()
                if r == 1:
                    units = [[(0, m * 512, (m + 1) * 512, 0)] for m in range(4)]
                elif r == 4:
                    units = [[(c, 0, 512, 0)] for c in range(4)]
                else:
                    units = [[(c0 + k, 0, 128, k * 128) for k in range(4)] for c0 in range(0, 16, 4)]
                steps = []
                for unit in units:
                    us = []
                    for (c, qlo, qhi, coff) in unit:
                        for j in range(tpc):
                            lo = max(128 * j - 64, qlo)
                            hi = min(128 * j + 192, qhi)
                            if hi > lo:
                                us.append([c, j, lo, hi, coff, qlo, False, False, unit])
                    us[0][6] = True
                    us[-1][7] = True
                    steps.extend(tuple(u) for u in us)
                prev = None
                ndset = None
                for stp in steps:
                    pk = stage_a(stp, r, L, tpc, bs)
                    if prev is not None:
                        stage_b(prev[0], prev[1], r, L, tpc, gi, prev[2])
                    if stp[6]:
                        ndset = nd_sets.next()
                    prev = (stp, pk, ndset)
                stage_b(prev[0], prev[1], r, L, tpc, gi, prev[2])
            pending_norm.append(hp)

        emit_norm()
        wo_banks = Rot([1, 3])
        self.resid_eng = "dve"
        for t in range(NT):
            b0 = wo_banks.next()
            for nh in range(2):
                for k in range(8):
                    P.mm(self.ps[b0 + nh], oT[:, k, t * 128:(t + 1) * 128], wo_ap[k][:, nh * 512:(nh + 1) * 512],
                         k == 0, k == 7, [oTb[k], wo_buf[k]], [self.psb[b0 + nh]])
            self.post_norm_residual(t, b0)


_CONSTS = None


def _t5_bucket(rel):
    rel = np.asarray(rel, dtype=np.int64)
    nb, max_exact = 16, 8
    base = np.where(rel > 0, nb, 0)
    n = np.abs(rel)
    nf = np.maximum(n, 1).astype(np.float32)
    lg = (np.log(nf / np.float32(max_exact)) / np.float32(np.log(1024 / max_exact)) * np.float32(nb - max_exact)).astype(np.float32)
    large = max_exact + lg.astype(np.int32)
    large = np.minimum(large, nb - 1)
    return base + np.where(n < max_exact, n, large)


def _consts():
    global _CONSTS
    if _CONSTS is None:
        oh = np.zeros((32, 3, 384), dtype=np.float32)
        mrow = np.full((1, 384), -30000.0, dtype=np.float32)
        for gi, r in enumerate((1, 4, 16)):
            for u in range(127, 256):
                delta = 191 - u
                oh[int(_t5_bucket(delta * r)), gi, u] = 1.0
        mrow[0, 127:256] = 0.0
        _CONSTS = {
            "c_ident": np.eye(128, dtype=np.float32).astype(ml_dtypes.bfloat16),
            "c_J": np.ascontiguousarray(np.eye(128, dtype=np.float32)[::-1]).astype(ml_dtypes.bfloat16),
            "c_oh": oh,
            "c_mrow": mrow,
        }
    return _CONSTS


_NC_CACHE = {}


def get_nc(phases):
    key = tuple(phases)
    if key not in _NC_CACHE:
        _NC_CACHE[key] = Prog(phases).build()
    return _NC_CACHE[key]


def run(inputs, phases, trace=False):
    nc = get_nc(phases)
    x = np.ascontiguousarray(inputs["x"], dtype=np.float32)
    shared = {k: np.ascontiguousarray(v) for k, v in inputs.items() if k != "x"}
    in_maps = []
    for c in range(8):
        m = {"x": x[c]}
        for k in ("norm_mix_pre", "norm_mix_post", "norm_mlp_pre", "norm_mlp_post", "mlp_w_up", "mlp_w_down"):
            m[k] = shared[k]
        if "conv" in phases:
            def colmaj(v):
                return np.ascontiguousarray(np.asarray(v, dtype=np.float32).reshape(-1, 128).T)
            m["conv_w_pw1"] = shared["conv_w_pw1"]
            m["conv_w_pw2"] = shared["conv_w_pw2"]
            m["conv_b_pw2"] = shared["conv_b_pw2"]
            m["l_conv_ln_g"] = colmaj(shared["conv_ln_g"][0])
            m["l_conv_ln_b"] = colmaj(shared["conv_ln_b"][0])
            m["l_conv_b_dw"] = colmaj(shared["conv_b_dw"][0])
            m["l_conv_b_pw1"] = colmaj(shared["conv_b_pw1"][0])
            m["l_conv_w_dw"] = np.ascontiguousarray(shared["conv_w_dw"][0].T.reshape(8, 128, 31))
        if "attn" in phases:
            m["attn_w_qkv"] = shared["attn_w_qkv"]
            m["attn_w_o"] = shared["attn_w_o"]
            m["rel_bias"] = shared["rel_bias"]
        cs = _consts()
        m["c_ident"] = cs["c_ident"]
        if "attn" in phases:
            for k in ("c_J", "c_oh", "c_mrow"):
                m[k] = cs[k]
        in_maps.append(m)
    res = run_bass_kernel_spmd(nc, in_maps, core_ids=list(range(8)), trace=trace)
    out = np.stack([np.asarray(r["y"]) for r in res.results], axis=0).astype(np.float32)
    return out, res


def kernel(**inputs):
    out, _ = run(inputs, ("attn", "mlp0", "conv", "mlp1"))
    return out
```

```python
import numpy as np
import ml_dtypes
import concourse.bass as bass
import concourse.mybir as mybir
from concourse.bass_utils import run_bass_kernel_spmd

F32 = mybir.dt.float32
BF16 = mybir.dt.bfloat16
AF = mybir.ActivationFunctionType
ALU = mybir.AluOpType

D = 1024
S_LEN = 2048
NT = 16
DFF = 4096
RMS_EPS = 1e-6
LN_EPS = 1e-5


class Ctr:
    def __init__(self, sem, step):
        self.sem = sem
        self.step = step
        self.ops = []


class Op:
    __slots__ = ("eng", "fn", "deps", "marked", "ctr", "val", "is_dma")

    def __init__(self, eng, fn, ctr, is_dma):
        self.eng = eng
        self.fn = fn
        self.deps = []
        self.marked = is_dma
        self.ctr = ctr
        self.val = None
        self.is_dma = is_dma


class Buf:
    __slots__ = ("writer", "readers", "name")

    def __init__(self, name=""):
        self.writer = None
        self.readers = {}
        self.name = name


class Sched:
    ENGS = ("pe", "act", "dve", "pool", "sp")

    def __init__(self, nc, new_sem):
        self.nc = nc
        self.new_sem = new_sem
        self.q = {e: [] for e in self.ENGS}
        self.ectr = {e: Ctr(new_sem("c_" + e), 1) for e in ("pe", "act", "dve", "pool")}
        self.nops = 0

    def dma_ctr(self, name):
        return Ctr(self.new_sem(name), 16)

    def op(self, eng, fn, reads=(), writes=(), dma=None):
        is_dma = dma is not None
        o = Op(eng, fn, dma if is_dma else self.ectr.get(eng), is_dma)
        deps = []
        for b in reads:
            if b.writer is not None:
                deps.append((b.writer, True))
        for b in writes:
            if b.writer is not None:
                deps.append((b.writer, False))
            for r in b.readers.values():
                deps.append((r, False))
        key = ("dma", id(dma)) if is_dma else eng
        for (d, raw) in deps:
            if d is o:
                continue
            if (not d.is_dma) and (not is_dma) and d.eng == eng:
                if (not raw) or eng == "pe":
                    continue
            d.marked = True
            o.deps.append(d)
        for b in reads:
            b.readers[key] = o
        for b in writes:
            b.writer = o
            b.readers = {}
        self.q[eng].append(o)
        self.nops += 1
        return o

    def wait_all(self, eng, ops):
        o = Op(eng, None, None, False)
        for d in ops:
            d.marked = True
            o.deps.append(d)
        self.q[eng].append(o)
        return o

    def barrier(self):
        lasts = []
        for e in ("pe", "act", "dve", "pool"):
            for o in reversed(self.q[e]):
                if o.fn is not None and not o.is_dma:
                    lasts.append(o)
                    break
        seen = set()
        for e in self.ENGS:
            for o in reversed(self.q[e]):
                if o.is_dma and id(o.ctr) not in seen:
                    seen.add(id(o.ctr))
                    lasts.append(o)
        for e in self.ENGS:
            self.wait_all(e, lasts)

    def finalize(self):
        for e in self.ENGS:
            for o in self.q[e]:
                if o.ctr is not None:
                    o.ctr.ops.append(o)
        seen = set()
        for e in self.ENGS:
            for o in self.q[e]:
                c = o.ctr
                if c is None or id(c) in seen:
                    continue
                seen.add(id(c))
                n = 0
                for p in c.ops:
                    if p.marked:
                        n += c.step
                    p.val = n

    def emit_engine(self, name, e):
        waited = {}
        for o in self.q[name]:
            need = {}
            for d in o.deps:
                k = id(d.ctr)
                if d.val > need.get(k, (None, 0))[1]:
                    need[k] = (d.ctr, d.val)
            for k, (c, v) in need.items():
                if waited.get(k, 0) >= v:
                    continue
                e.wait_ge(c.sem, v)
                waited[k] = v
            if o.fn is None:
                continue
            ins = o.fn(e)
            if o.marked:
                ins.then_inc(o.ctr.sem, o.ctr.step)

    def emit(self):
        self.finalize()
        with self.nc.Block() as block:
            @block.tensor
            def _(e):
                self.emit_engine("pe", e)

            @block.scalar
            def _(e):
                self.emit_engine("act", e)

            @block.vector
            def _(e):
                self.emit_engine("dve", e)

            @block.gpsimd
            def _(e):
                self.emit_engine("pool", e)

            @block.sync
            def _(e):
                self.emit_engine("sp", e)


class Rot:
    def __init__(self, items):
        self.items = list(items)
        self.i = 0

    def next(self):
        it = self.items[self.i % len(self.items)]
        self.i += 1
        return it


class Prog:
    def __init__(self, phases):
        self.phases = phases
        self.nc = bass.Bass("TRN2", target_bir_lowering=False)
        self.stack = []
        self.S = None

    def enter(self, cm):
        v = cm.__enter__()
        self.stack.append(cm)
        return v

    def sem(self, name):
        return self.enter(self.nc.semaphore(name))

    def sb(self, name, shape, dt):
        return self.enter(self.nc.sbuf_tensor(name, shape, dt))

    def close(self):
        while self.stack:
            self.stack.pop().__exit__(None, None, None)

    def dram_in(self, name, shape, dt=F32):
        return self.nc.dram_tensor(name, list(shape), dt, kind="ExternalInput").ap()

    def mm(self, out, lhsT, rhs, start, stop, reads, writes, **kw):
        return self.S.op("pe", lambda e: e.matmul(out, lhsT, rhs, start=start, stop=stop, **kw),
                         reads, writes)

    def tr(self, out, in_, ident, reads, writes):
        return self.S.op("pe", lambda e: e.transpose(out, in_, ident), reads, writes)

    def act(self, out, in_, func, reads, writes, **kw):
        return self.S.op("act", lambda e: e.activation(out, in_, func, **kw), reads, writes)

    def dma(self, eng, out, in_, ctr, reads, writes):
        return self.S.op(eng, lambda e: e.dma_start(out=out, in_=in_), reads, writes, dma=ctr)

    def v(self, eng, fn, reads, writes):
        return self.S.op(eng, fn, reads, writes)

    def build(self):
        nc = self.nc
        P = self
        self.x_d = P.dram_in("x", [S_LEN, D])
        self.y_d = nc.dram_tensor("y", [S_LEN, D], F32, kind="ExternalOutput").ap()
        self.g_mix_pre = P.dram_in("norm_mix_pre", [2, D])
        self.g_mix_post = P.dram_in("norm_mix_post", [2, D])
        self.g_mlp_pre = P.dram_in("norm_mlp_pre", [2, D])
        self.g_mlp_post = P.dram_in("norm_mlp_post", [2, D])
        self.w_up = P.dram_in("mlp_w_up", [2, D, DFF])
        self.w_dn = P.dram_in("mlp_w_down", [2, DFF, D])
        self.ident_d = P.dram_in("c_ident", [128, 128], BF16)
        if "attn" in self.phases:
            self.w_qkv = P.dram_in("attn_w_qkv", [1, D, 9216])
            self.w_o = P.dram_in("attn_w_o", [1, D, D])
            self.rel_bias = P.dram_in("rel_bias", [32, 48])
            self.j_d = P.dram_in("c_J", [128, 128], BF16)
            self.oh_d = P.dram_in("c_oh", [32, 3, 384])
            self.mrow_d = P.dram_in("c_mrow", [1, 384])
        if "conv" in self.phases:
            self.c_w_pw1 = P.dram_in("conv_w_pw1", [1, D, 2 * D])
            self.c_w_pw2 = P.dram_in("conv_w_pw2", [1, D, D])
            self.c_b_pw2 = P.dram_in("conv_b_pw2", [1, D])
            self.c_ln_g = P.dram_in("l_conv_ln_g", [128, 8])
            self.c_ln_b = P.dram_in("l_conv_ln_b", [128, 8])
            self.c_b_dw = P.dram_in("l_conv_b_dw", [128, 8])
            self.c_b_pw1 = P.dram_in("l_conv_b_pw1", [128, 16])
            self.c_w_dw = P.dram_in("l_conv_w_dw", [8, 128, 31])

        self.S = Sched(nc, self.sem)
        S = self.S

        self.x_sb = P.sb("x_sb", [128, NT, D], F32)
        self.xb = [Buf("x%d" % t) for t in range(NT)]
        self.ident = P.sb("ident", [128, 128], BF16)
        self.identb = Buf("ident")
        self.gt = [P.sb("gain%d" % k, [128, D], F32) for k in range(2)]
        self.gtb = [Buf("g0"), Buf("g1")]
        self.gctr = [S.dma_ctr("gc0"), S.dma_ctr("gc1")]
        self.eps_t = P.sb("eps_t", [128, 4], F32)
        self.eps_ap = self.eps_t[:, 0:1]
        self.lneps_ap = self.eps_t[:, 1:2]
        epsb = Buf("eps")
        P.v("pool", lambda e: e.memset(self.eps_t[:, 0:1], RMS_EPS), [], [epsb])
        P.v("pool", lambda e: e.memset(self.eps_t[:, 1:2], LN_EPS), [], [epsb])
        P.v("pool", lambda e: e.memset(self.eps_t[:, 2:3], -0.5), [], [epsb])
        self.epsb = epsb
        self.junk = P.sb("junk", [128, D], BF16)
        self.junkb = Buf("junk")
        self.ss = [P.sb("ss%d" % k, [128, 4], F32) for k in range(4)]
        self.ssb = [Buf("ss%d" % k) for k in range(4)]
        self.ssrot = 0
        self.htok = [P.sb("htok%d" % k, [128, D], BF16) for k in range(2)]
        self.htokb = [Buf("ht0"), Buf("ht1")]
        self.htrot = 0
        self.tmp = [P.sb("tmp%d" % k, [128, D], F32) for k in range(1)]
        self.tmpb = [Buf("tmp0")]
        self.tmprot = 0
        self.psum = self.enter(nc.psum_tensor("psum", [128, 8, 512], F32))
        self.ps = [self.psum[:, k, :] for k in range(8)]
        self.psb = [Buf("ps%d" % k) for k in range(8)]

        c0 = S.dma_ctr("cident")
        P.dma("sp", self.ident[:, :], self.ident_d[:, :], c0, [], [self.identb])
        self.xctr = [S.dma_ctr("xc%d" % t) for t in range(NT)]
        self.pending_x = []
        for t in range(NT):
            f = (lambda t: (lambda: P.dma("sp", self.x_sb[:, t, :], self.x_d[t * 128:(t + 1) * 128, :], self.xctr[t], [], [self.xb[t]])))(t)
            if t < 4:
                f()
            else:
                self.pending_x.append(f)

        self.wub = nc.dram_tensor("wub_scratch", [2, D, DFF], BF16).ap()
        self.cast_q = []

        self.store_ops = []
        self.store_now = False
        self.resid_eng = "pool"
        self.rstd_mode = "pool"
        for pi, ph in enumerate(self.phases):
            self.store_now = (pi == len(self.phases) - 1)
            mark = len(self.stack)
            if ph in ("mlp0", "mlp1"):
                self.rstd_mode, self.resid_eng = "act", "dve"
                self.mlp(int(ph[-1]))
                self.rstd_mode, self.resid_eng = "pool", "pool"
            elif ph == "attn":
                self.attn()
            elif ph == "conv":
                self.conv()
            else:
                raise ValueError(ph)
            S.barrier()
            while len(self.stack) > mark:
                self.stack.pop().__exit__(None, None, None)

        assert len(self.store_ops) == NT
        S.wait_all("sp", self.store_ops)
        S.emit()
        self.close()
        return nc

    def pump(self, n):
        for _ in range(n):
            if self.cast_q:
                self.cast_q.pop(0)()

    def load_gain(self, k, g_dram_row):
        P = self
        P.dma("sp", self.gt[k][:, :], g_dram_row.partition_broadcast(128), self.gctr[k], [], [self.gtb[k]])

    def rstd_from_ss(self, ss_ap, ssbuf):
        P = self
        r = ss_ap[:, 3:4]
        q = ss_ap[:, 2:3]
        if self.rstd_mode == "pool":
            P.v("pool", lambda e: e.tensor_scalar(q, ss_ap[:, 0:1], 1.0 / D, RMS_EPS, ALU.mult, ALU.add), [ssbuf], [ssbuf])
            P.v("pool", lambda e: e.tensor_tensor(r, q, self.eps_t[:, 2:3], ALU.pow), [ssbuf, self.epsb], [ssbuf])
        else:
            P.act(q, ss_ap[:, 0:1], AF.Sqrt, [ssbuf, self.epsb], [ssbuf], scale=1.0 / D, bias=self.eps_ap)
            P.v("dve", lambda e: e.reciprocal(r, q), [ssbuf], [ssbuf])
        return r

    def norm_A(self, t):
        P = self
        x_t = self.x_sb[:, t, :]
        k = self.ssrot % 4
        self.ssrot += 1
        ss, ssb = self.ss[k], self.ssb[k]
        P.act(self.junk[:, :], x_t, AF.Square, [self.xb[t]], [self.junkb, ssb], accum_out=ss[:, 0:1])
        r = self.rstd_from_ss(ss, ssb)
        hk = self.htrot % 2
        self.htrot += 1
        h, hb = self.htok[hk], self.htokb[hk]
        g = self.gt[0]
        P.v("dve", lambda e: e.scalar_tensor_tensor(h[:, :], x_t, r, g[:, :], ALU.mult, ALU.mult),
            [self.xb[t], ssb, self.gtb[0]], [hb])
        return h, hb

    def norm_B(self, hhb, hT_ap, hT_buf, tr_bank, use_act=True):
        P = self
        h, hb = hhb
        bank = self.ps[tr_bank]
        bb = self.psb[tr_bank]
        pv = bank.bitcast(BF16)
        for kc in range(8):
            P.tr(pv[:, kc * 128:(kc + 1) * 128], h[:, kc * 128:(kc + 1) * 128], self.ident[:, :],
                 [hb, self.identb], [bb])
        if use_act:
            P.act(hT_ap, pv.rearrange("p (k c) -> p k c", k=8), AF.Copy, [bb], [hT_buf])
        else:
            P.v("dve", lambda e: e.tensor_copy(hT_ap, pv.rearrange("p (k c) -> p k c", k=8)), [bb], [hT_buf])

    def norm_pre_many(self, tiles, dst_fn, tr_banks=(0,)):
        tiles = list(tiles)
        self.norm_calls = getattr(self, "norm_calls", 0) + 1
        while self.pending_x and (self.norm_calls >= 2 or max(tiles) >= 4):
            self.pending_x.pop(0)()
        prev = None
        for n, t in enumerate(tiles):
            cur = (self.norm_A(t), t, n)
            if prev is not None:
                ap, bf = dst_fn(prev[1])
                self.norm_B(prev[0], ap, bf, tr_banks[prev[2] % len(tr_banks)], prev[2] % 2 == 0)
            prev = cur
        ap, bf = dst_fn(prev[1])
        self.norm_B(prev[0], ap, bf, tr_banks[prev[2] % len(tr_banks)], prev[2] % 2 == 0)

    def post_norm_residual(self, t, bankA=None, m=None, mb=None):
        P = self
        k = self.ssrot % 4
        self.ssrot += 1
        ss, ssb = self.ss[k], self.ssb[k]
        if m is None:
            m = self.psum[:, bankA:bankA + 2, :]
            mb = [self.psb[bankA], self.psb[bankA + 1]]
        P.act(self.junk[:, :].rearrange("p (a b) -> p a b", a=2), m, AF.Square, mb, [self.junkb, ssb],
              accum_out=ss[:, 0:1])
        r = self.rstd_from_ss(ss, ssb)
        tmp, tb = self.tmp[0], self.tmpb[0]
        g = self.gt[1]
        P.v("dve", lambda e: e.scalar_tensor_tensor(tmp[:, :].rearrange("p (a b) -> p a b", a=2), m, r,
                                                    g[:, :].rearrange("p (a b) -> p a b", a=2), ALU.mult, ALU.mult),
            mb + [ssb, self.gtb[1]], [tb])
        x_t = self.x_sb[:, t, :]
        P.v(self.resid_eng, lambda e: e.tensor_tensor(x_t, x_t, tmp[:, :], ALU.add), [tb, self.xb[t]], [self.xb[t]])
        if self.store_now:
            self.store_ops.append(P.dma("sp", self.y_d[t * 128:(t + 1) * 128, :], x_t, self.xctr[t], [self.xb[t]], []))

    def mlp(self, i):
        P = self
        S = self.S
        nc = self.nc
        TB = 512
        NB = S_LEN // TB
        wdn = P.sb("wdn%d" % i, [128, 32, D], BF16)
        wdnb = [Buf("wdn%d" % s) for s in range(8)]
        wdnc = [S.dma_ctr("wdnc%d_%d" % (i, s)) for s in range(8)]
        uT = P.sb("uT%d" % i, [128, 32, TB], BF16)
        uTb = [Buf("uT%d" % j) for j in range(32)]
        hT = [P.sb("hTm%d_%d" % (i, k), [128, 8, TB], BF16) for k in range(1)]
        hTb = [[Buf("hTm%d_%d" % (k, q)) for q in range(4)] for k in range(1)]
        NSLOT = 3
        wup = [P.sb("wup%d_%d" % (i, k), [128, 8, 256], BF16) for k in range(NSLOT)]
        wupb = [Buf("wup%d" % k) for k in range(NSLOT)]
        wupc = [S.dma_ctr("wupc%d_%d" % (i, k)) for k in range(NSLOT)]
        rt = [P.sb("rt%d_%d" % (i, k), [128, 512], BF16) for k in range(2)]
        rtb = [Buf("rt0"), Buf("rt1")]

        self.load_gain(0, self.g_mlp_pre[i:i + 1, :])
        self.load_gain(1, self.g_mlp_post[i:i + 1, :])
        wdn_src = self.w_dn[i].rearrange("(j p) n -> p j n", p=128)
        scrb = [Buf("scr%d" % jg) for jg in range(16)]
        scrc = [S.dma_ctr("scrc%d_%d" % (i, jg)) for jg in range(16)]
        wup_f32 = self.w_up[i].rearrange("(k p) n -> p k n", p=128)
        wup_src = self.wub[i].rearrange("(k p) n -> p k n", p=128)

        up_banks = Rot([1, 2, 3])
        dn_banks = Rot([4, 6])
        slot_i = 0
        rt_i = 0
        for tb in range(NB):
            hk = 0
            self.norm_pre_many(range(tb * 4, tb * 4 + 4),
                               lambda t: (hT[hk][:, :, (t % 4) * 128:(t % 4 + 1) * 128], hTb[hk][t % 4]))
            for jg in range(16):
                sl = slot_i % NSLOT
                slot_i += 1
                cols = slice(jg * 256, (jg + 1) * 256)
                if tb == 0:
                    P.dma("pool", wup[sl][:, :, :], wup_f32[:, :, cols], wupc[sl], [], [wupb[sl]])
                    P.dma("sp", wup_src[:, :, cols], wup[sl][:, :, :], scrc[jg], [wupb[sl]], [scrb[jg]])
                    if jg % 2 == 1:
                        s8 = jg // 2
                        P.dma("pool", wdn[:, s8 * 4:(s8 + 1) * 4, :], wdn_src[:, s8 * 4:(s8 + 1) * 4, :], wdnc[s8], [], [wdnb[s8]])
                else:
                    P.dma("sp", wup[sl][:, :, :], wup_src[:, :, cols], wupc[sl], [scrb[jg]], [wupb[sl]])
                for j in range(2):
                    bk = up_banks.next()
                    for kc in range(8):
                        P.mm(self.ps[bk], wup[sl][:, kc, j * 128:(j + 1) * 128], hT[hk][:, kc, :],
                             kc == 0, kc == 7, [wupb[sl]] + hTb[hk], [self.psb[bk]])
                    rk = rt_i % 2
                    rt_i += 1
                    P.act(rt[rk][:, :], self.ps[bk], AF.Relu, [self.psb[bk]], [rtb[rk]])
                    jj = jg * 2 + j
                    P.v("dve", (lambda o, a: lambda e: e.tensor_tensor(o, a, a, ALU.mult))(uT[:, jj, :], rt[rk][:, :]),
                        [rtb[rk]], [uTb[jj]])
            for q in range(4):
                t = tb * 4 + q
                b0 = dn_banks.next()
                for nh in range(2):
                    bk = b0 + nh
                    for j in range(32):
                        P.mm(self.ps[bk], uT[:, j, q * 128:(q + 1) * 128], wdn[:, j, nh * 512:(nh + 1) * 512],
                             j == 0, j == 31, [uTb[j], wdnb[j // 4]], [self.psb[bk]])
                self.post_norm_residual(t, b0)


    def conv(self):
        P = self
        S = self.S
        nc = self.nc
        LI = 1
        PAD = 16
        hT = P.sb("c_hT", [128, 8, S_LEN], BF16)
        hTb = [Buf("c_hT%d" % t) for t in range(NT)]
        uT = P.sb("c_uT", [128, 8, S_LEN + 2 * PAD], BF16)
        uTb = [[Buf("c_uT%d_%d" % (c, g)) for g in range(4)] for c in range(8)]
        padb = [Buf("c_pad%d" % c) for c in range(8)]
        w1 = [P.sb("c_w1_%d" % k, [128, 8, 2, 128], BF16) for k in range(2)]
        w1b = [Buf("w1_0"), Buf("w1_1")]
        w1c = [S.dma_ctr("w1c0"), S.dma_ctr("w1c1")]
        w2 = P.sb("c_w2", [128, 8, D], BF16)
        w2b = Buf("w2")
        w2c = S.dma_ctr("w2c")
        diag = [P.sb("c_diag%d" % k, [128, 31, 128], BF16) for k in range(1)]
        diagb = [Buf("dg0")]
        sm = P.sb("c_small", [128, 40], F32)
        smb = Buf("c_small")
        smc = S.dma_ctr("smc")
        wdw = P.sb("c_wdw", [128, 8, 31], F32)
        wdwb = Buf("wdw")
        brow = P.sb("c_brow", [1, D], F32)
        bh = P.sb("c_bh", [1, D], BF16)
        bl = P.sb("c_bl", [1, D], BF16)
        ones_rb = P.sb("c_ones_rb", [1, 128], BF16)
        browb = Buf("brow")
        vsq = [P.sb("c_vsq%d" % k, [128, 512], BF16) for k in range(2)]
        vsqb = [Buf("vsq0"), Buf("vsq1")]
        yt = [P.sb("c_yt%d" % k, [128, 512], F32) for k in range(2)]
        ytb = [Buf("yt0"), Buf("yt1")]
        sig, sigb = yt, ytb
        row = [P.sb("c_row%d" % k, [1, 2, 512], F32) for k in range(2)]
        rowb = [Buf("row0"), Buf("row1")]
        ones_c = P.sb("c_ones_c", [128, 1], BF16)
        ones_r = P.sb("c_ones_r", [1, 128], F32)
        onesb = Buf("ones")

        self.load_gain(0, self.g_mix_pre[LI:LI + 1, :])
        self.load_gain(1, self.g_mix_post[LI:LI + 1, :])
        P.dma("sp", sm[:, 0:8], self.c_ln_g[:, :], smc, [], [smb])
        P.dma("sp", sm[:, 8:16], self.c_ln_b[:, :], smc, [], [smb])
        P.dma("sp", sm[:, 16:24], self.c_b_dw[:, :], smc, [], [smb])
        P.dma("sp", sm[:, 24:40], self.c_b_pw1[:, :], smc, [], [smb])
        P.dma("sp", wdw[:, :, :], self.c_w_dw.rearrange("c p j -> p c j"), smc, [], [wdwb])
        P.dma("sp", brow[:, :], self.c_b_pw2[0:1, :], smc, [], [browb])
        P.v("pool", lambda e: e.memset(ones_rb[:, :], 1.0), [], [onesb])
        P.v("dve", lambda e: e.tensor_copy(bh[:, :], brow[:, :]), [browb], [browb])
        P.v("dve", lambda e: e.tensor_tensor(brow[:, :], brow[:, :], bh[:, :], ALU.subtract), [browb], [browb])
        P.v("dve", lambda e: e.tensor_copy(bl[:, :], brow[:, :]), [browb], [browb])
        P.v("pool", lambda e: e.memset(ones_c[:, :], 1.0), [], [onesb])
        P.v("pool", lambda e: e.memset(ones_r[:, :], 1.0), [], [onesb])
        for c in range(8):
            P.v("pool", (lambda c: lambda e: e.memset(uT[:, c, 0:PAD], 0.0))(c), [], [padb[c]])
            P.v("pool", (lambda c: lambda e: e.memset(uT[:, c, PAD + S_LEN:], 0.0))(c), [], [padb[c]])
        w2_src = self.c_w_pw2[0].rearrange("(k p) n -> p k n", p=128)

        w1_src = self.c_w_pw1[0].rearrange("(k p) n -> p k n", p=128)
        a_banks = Rot([1, 2])
        g_banks = Rot([3, 4])
        si = 0
        for j in range(8):
            sl = j % 2
            P.dma("pool", w1[sl][:, :, 0, :], w1_src[:, :, j * 128:(j + 1) * 128], w1c[sl], [], [w1b[sl]])
            P.dma("pool", w1[sl][:, :, 1, :], w1_src[:, :, D + j * 128:D + (j + 1) * 128], w1c[sl], [], [w1b[sl]])
            if j == 1:
                P.dma("pool", w2[:, :, :], w2_src, w2c, [], [w2b])
            for tg in range(4):
                if j == 0:
                    self.norm_pre_many(range(tg * 4, tg * 4 + 4), lambda t: (hT[:, :, t * 128:(t + 1) * 128], hTb[t]), tr_banks=(0, 7))
                ba = a_banks.next()
                bg = g_banks.next()
                hb = hTb[tg * 4:(tg + 1) * 4]
                for kc in range(8):
                    P.mm(self.ps[bg], w1[sl][:, kc, 1, :], hT[:, kc, tg * 512:(tg + 1) * 512], kc == 0, kc == 7,
                         [w1b[sl]] + hb, [self.psb[bg]])
                for kc in range(8):
                    P.mm(self.ps[ba], w1[sl][:, kc, 0, :], hT[:, kc, tg * 512:(tg + 1) * 512], kc == 0, kc == 7,
                         [w1b[sl]] + hb, [self.psb[ba]])
                sk = si % 2
                si += 1
                P.act(sig[sk][:, :], self.ps[bg], AF.Sigmoid, [self.psb[bg], smb], [sigb[sk]],
                      bias=sm[:, 24 + 8 + j:24 + 8 + j + 1])
                o = uT[:, j, PAD + tg * 512:PAD + (tg + 1) * 512]
                P.v("dve", (lambda o, pa, bb, sg: lambda e: e.scalar_tensor_tensor(o, pa, bb, sg, ALU.add, ALU.mult))(
                    o, self.ps[ba], sm[:, 24 + j:24 + j + 1], sig[sk][:, :]),
                    [self.psb[ba], smb, sigb[sk]], [uTb[j][tg]])

        c_banks = Rot([5, 6, 7])
        for c in range(8):
            dk = 0
            for j in range(31):
                P.v("dve", (lambda o, w: lambda e: e.tensor_scalar(o, self.ident[:, :], w, None, ALU.mult))(
                    diag[dk][:, j, :], wdw[:, c, j:j + 1]), [self.identb, wdwb], [diagb[dk]])
            for tg in range(4):
                bk = c_banks.next()
                for j in range(31):
                    off = PAD + tg * 512 + j - 15
                    rb = [uTb[c][tg]]
                    if j < 15:
                        rb.append(uTb[c][tg - 1] if tg > 0 else padb[c])
                    if j > 15:
                        rb.append(uTb[c][tg + 1] if tg < 3 else padb[c])
                    P.mm(self.ps[bk], diag[dk][:, j, :], uT[:, c, off:off + 512], j == 0, j == 30,
                         [diagb[dk]] + rb, [self.psb[bk]])
                P.act(hT[:, c, tg * 512:(tg + 1) * 512], self.ps[bk], AF.Identity, [self.psb[bk], smb],
                      hTb[tg * 4:(tg + 1) * 4], bias=sm[:, 16 + c:16 + c + 1])

        BS1, BS2, BA, BB = 2, 3, 4, 5
        pw2_banks = Rot([6, 0])
        stc = {"vi": 0, "yi": 0}

        def ln_stats(tg):
            hb = hTb[tg * 4:(tg + 1) * 4]
            for c in range(8):
                vk = stc["vi"] % 2
                stc["vi"] += 1
                vv = hT[:, c, tg * 512:(tg + 1) * 512]
                P.act(vsq[vk][:, :], vv, AF.Square, hb, [vsqb[vk]])
                P.mm(self.ps[BS1][0:1, :], ones_c[:, :], vv, c == 0, c == 7, hb + [onesb], [self.psb[BS1]])
                P.mm(self.ps[BS2][0:1, :], ones_c[:, :], vsq[vk][:, :], c == 0, c == 7, [vsqb[vk], onesb], [self.psb[BS2]])

        def ln_rowmath(tg):
            k = tg % 2
            Ar, Br = row[k][:, 0, :], row[k][:, 1, :]
            rb = rowb[k]
            s1, s2 = self.ps[BS1][0:1, :], self.ps[BS2][0:1, :]
            P.v("dve", lambda e: e.tensor_scalar(Br, s1, 1.0 / D, None, ALU.mult), [self.psb[BS1]], [rb])
            P.v("dve", lambda e: e.tensor_tensor(Ar, Br, Br, ALU.mult), [rb], [rb])
            P.v("dve", lambda e: e.scalar_tensor_tensor(Ar, s2, 1.0 / D, Ar, ALU.mult, ALU.subtract),
                [self.psb[BS2], rb], [rb])
            P.act(Ar, Ar, AF.Sqrt, [rb, self.epsb], [rb], bias=self.lneps_ap[0:1, :])
            P.v("dve", lambda e: e.reciprocal(Ar, Ar), [rb], [rb])
            P.v("dve", lambda e: e.scalar_tensor_tensor(Br, Br, -1.0, Ar, ALU.mult, ALU.mult), [rb], [rb])

        def ln_bcast(tg):
            k = tg % 2
            P.mm(self.ps[BA], ones_r[:, :], row[k][:, 0, :], True, True, [rowb[k], onesb], [self.psb[BA]])
            P.mm(self.ps[BB], ones_r[:, :], row[k][:, 1, :], True, True, [rowb[k], onesb], [self.psb[BB]])

        def ln_apply(tg, cs):
            hb = hTb[tg * 4:(tg + 1) * 4]
            for c in cs:
                yk = stc["yi"] % 2
                stc["yi"] += 1
                vv = hT[:, c, tg * 512:(tg + 1) * 512]
                y = yt[yk][:, :]
                P.v("dve", (lambda y, vv: lambda e: e.tensor_tensor(y, vv, self.ps[BA], ALU.mult))(y, vv),
                    hb + [self.psb[BA]], [ytb[yk]])
                P.v("dve", (lambda y: lambda e: e.tensor_tensor(y, y, self.ps[BB], ALU.add))(y),
                    [ytb[yk], self.psb[BB]], [ytb[yk]])
                P.act(uT[:, c, PAD + tg * 512:PAD + (tg + 1) * 512], y, AF.Silu, [ytb[yk], smb], [uTb[c][tg]],
                      scale=sm[:, c:c + 1], bias=sm[:, 8 + c:8 + c + 1])

        def pw2_tile(tg, q):
            if True:
                t = tg * 4 + q
                b0 = pw2_banks.next()
                for nh in range(2):
                    for c in range(8):
                        P.mm(self.ps[b0 + nh], uT[:, c, PAD + t * 128:PAD + (t + 1) * 128], w2[:, c, nh * 512:(nh + 1) * 512],
                             c == 0, False, [uTb[c][tg], w2b], [self.psb[b0 + nh]])
                    P.mm(self.ps[b0 + nh], ones_rb[:, :], bh[:, nh * 512:(nh + 1) * 512], False, False,
                         [onesb, browb], [self.psb[b0 + nh]])
                    P.mm(self.ps[b0 + nh], ones_rb[:, :], bl[:, nh * 512:(nh + 1) * 512], False, True,
                         [onesb, browb], [self.psb[b0 + nh]])
                self.post_norm_residual(t, b0)

        self.resid_eng = "dve"
        ln_stats(0)
        ln_rowmath(0)
        ln_bcast(0)
        for tg in range(5):
            if tg + 1 < 4:
                ln_stats(tg + 1)
            for q in range(4):
                if tg < 4:
                    ln_apply(tg, (2 * q, 2 * q + 1))
                if tg >= 1:
                    pw2_tile(tg - 1, q)
            if tg + 1 < 4:
                ln_rowmath(tg + 1)
                ln_bcast(tg + 1)
        self.resid_eng = "pool"

    def attn(self):
        P = self
        S = self.S
        nc = self.nc
        LI = 0
        GROUPS = (1, 4, 16)
        hT = P.sb("a_hT", [128, 8, S_LEN], BF16)
        hTb = [Buf("a_hT%d" % t) for t in range(NT)]
        oT = P.sb("a_oT", [128, 8, S_LEN], BF16)
        oTb = [Buf("a_oT%d" % k) for k in range(8)]
        wsl = [P.sb("a_w%d" % k, [128, 8, 3, 128], BF16) for k in range(2)]
        wslb = [Buf("a_w0"), Buf("a_w1")]
        wslc = [S.dma_ctr("a_wc0"), S.dma_ctr("a_wc1")]
        qZ = P.sb("a_qZ", [128, 2, S_LEN], BF16)
        kT = P.sb("a_kT", [128, S_LEN], BF16)
        vT = P.sb("a_vT", [128, S_LEN], BF16)
        qZb, kTb, vTb = Buf("qZ"), Buf("kT"), Buf("vT")
        VZ = P.sb("a_VZ", [128, 16, 2, 128], BF16)
        VZb = Buf("VZ")
        OZ = P.sb("a_OZ", [128, 2, 128], BF16)
        OZb = Buf("OZ")
        accw = P.sb("a_acc", [128, 2 * S_LEN], F32)
        nacc = accw[:, 0:S_LEN]
        dacc = accw[:, S_LEN:2 * S_LEN]
        naccb, daccb = Buf("nacc"), Buf("dacc")
        PT = [P.sb("a_PT%d" % k, [128, 2, 256], BF16) for k in range(3)]
        PTb = [Buf("PT%d" % k) for k in range(3)]
        BM = [P.sb("a_BM%d" % k, [128, 2, 256], BF16) for k in range(2)]
        BMb = [Buf("BM0"), Buf("BM1")]
        BMc = [S.dma_ctr("bmc0"), S.dma_ctr("bmc1")]
        jmat = P.sb("a_J", [128, 128], BF16)
        jb = Buf("J")
        relb = P.sb("a_relb", [32, 48], F32)
        ones16 = P.sb("a_ones16", [1, 16], F32)
        neg1 = P.sb("a_neg1", [128, 1], F32)
        oh = accw[0:32, 0:1152].rearrange("p (g u) -> p g u", g=3)
        mrow = accw[0:1, 1152:1536]
        gsb = accw[0:16, 1536:2112].bitcast(BF16).rearrange("p (g u) -> p g u", g=3)
        cb = Buf("a_consts")
        gsbb = Buf("gsb")
        cc = S.dma_ctr("a_cc")
        gd_t = nc.dram_tensor("a_gscratch", [48, 384], BF16)
        gd = gd_t.ap()
        gdb = Buf("gd")
        gdc = S.dma_ctr("a_gdc")

        self.load_gain(0, self.g_mix_pre[LI:LI + 1, :])
        self.load_gain(1, self.g_mix_post[LI:LI + 1, :])
        P.dma("sp", jmat[:, :], self.j_d[:, :], cc, [], [jb])
        P.dma("sp", relb[:, :], self.rel_bias[:, :], cc, [], [cb])
        P.dma("sp", oh, self.oh_d[:, :, :], cc, [], [cb])
        P.dma("sp", mrow, self.mrow_d[:, :], cc, [], [cb])
        P.v("pool", lambda e: e.memset(ones16[:, :], 1.0), [], [cb])
        P.v("pool", lambda e: e.memset(neg1[:, :], -1.0), [], [cb])
        P.v("pool", lambda e: e.memset(VZ[:, :, :, :], 0.0), [], [VZb])
        P.v("pool", lambda e: e.memset(qZ[:, :, :], 0.0), [], [qZb])
        P.v("pool", lambda e: e.memset(OZ[:, :, :], 0.0), [], [OZb])
        P.v("pool", lambda e: e.memset(OZ[:, 0, 0:64], 1.0), [OZb], [OZb])
        P.v("pool", lambda e: e.memset(OZ[:, 1, 64:128], 1.0), [OZb], [OZb])

        for g in range(3):
            bk = 1 + g
            P.mm(self.ps[bk][0:16, 0:384], relb[:, g * 16:(g + 1) * 16], oh[:, g, :], True, False, [cb], [self.psb[bk]])
            P.mm(self.ps[bk][0:16, 0:384], ones16[:, :], mrow, False, True, [cb], [self.psb[bk]])
            P.act(gsb[:, g, :], self.ps[bk][0:16, 0:384], AF.Copy, [self.psb[bk]], [gsbb])
        P.dma("sp", gd.rearrange("(g h) u -> h g u", g=3), gsb, gdc, [gsbb], [gdb])

        wq_src = self.w_qkv[0].rearrange("(k p) n -> p k n", p=128)
        qkv_banks = Rot([1, 2])
        st_banks = Rot([3, 4])
        nd_sets = Rot([(5, 6), (7, 0)])
        st = {"wi": 0, "bi": 0, "pti": 0}

        def stage_a(stp, r, L, tpc, bs):
            (c, j, lo, hi, coff, qlo, first, last, unit) = stp
            w = hi - lo
            woff = lo - (128 * j - 64)
            bst = st_banks.next()
            base = c * L
            o3 = self.ps[bst].rearrange("p (h x) -> p h x", h=2)[:, :, 0:w]
            P.mm(o3, kT[:, base + 128 * j:base + 128 * j + 128], qZ[:, :, base + lo:base + hi], True, False,
                 [kTb, qZb], [self.psb[bst]], skip_group_check=True)
            P.mm(o3, jmat[:, :], BM[bs][:, :, woff:woff + w], False, True, [jb, BMb[bs]], [self.psb[bst]],
                 skip_group_check=True)
            pk = st["pti"] % 3
            st["pti"] += 1
            P.act(PT[pk][:, :, 0:w], o3, AF.Exp, [self.psb[bst]], [PTb[pk]])
            return pk

        def stage_b(stp, pk, r, L, tpc, gi, ndset):
            (c, j, lo, hi, coff, qlo, first, last, unit) = stp
            bN, bD = ndset
            w = hi - lo
            ti = c * tpc + j
            c0 = coff + lo - qlo
            for h in range(2):
                P.mm(self.ps[bN][:, c0:c0 + w], VZ[:, ti, h, :], PT[pk][:, h, 0:w], first and h == 0, False,
                     [VZb, PTb[pk]], [self.psb[bN]], skip_group_check=True)
            for h in range(2):
                P.mm(self.ps[bD][:, c0:c0 + w], OZ[:, h, :], PT[pk][:, h, 0:w], first and h == 0, False,
                     [OZb, PTb[pk]], [self.psb[bD]], skip_group_check=True)
            if last:
                (c_first, uqlo, uqhi, _) = unit[0]
                ncl = len(unit)
                wq = uqhi - uqlo
                for (acc, accb, bk) in ((nacc, naccb, bN), (dacc, daccb, bD)):
                    view = acc.rearrange("p (l r) -> p r l", r=r)[:, c_first:c_first + ncl, uqlo:uqhi]
                    pin = self.ps[bk][:, 0:ncl * wq].rearrange("p (c l) -> p c l", c=ncl)
                    if gi == 0:
                        P.v("dve", (lambda o, i_: lambda e: e.tensor_copy(o, i_))(view, pin), [self.psb[bk]], [accb])
                    else:
                        P.v("dve", (lambda o, i_: lambda e: e.tensor_tensor(o, i_, o, ALU.add))(view, pin),
                            [self.psb[bk], accb], [accb])

        def load_w(it):
            hp_, gi_ = it // 3, it % 3
            sl_ = it % 2
            for s3 in range(3):
                col = (gi_ * 3 + s3) * 1024 + hp_ * 128
                P.dma("pool", wsl[sl_][:, :, s3, :], wq_src[:, :, col:col + 128], wslc[sl_], [], [wslb[sl_]])

        def load_bm(it):
            hp_, gi_ = it // 3, it % 3
            bs_ = it % 2
            src = bass.AP(gd_t, (gi_ * 16 + 2 * hp_) * 384, [[1, 128], [384, 2], [1, 256]])
            P.dma("sp", BM[bs_][:, :, :], src, BMc[bs_], [gdb], [BMb[bs_]])

        wo_ap, wo_buf = [], []

        def load_wo():
            wo_src = self.w_o[0].rearrange("(k p) n -> p k n", p=128)
            w0f = wsl[0][:, :, :, :].rearrange("p a b c -> p (a b c)")
            w1f = wsl[1][:, :, :, :].rearrange("p a b c -> p (a b c)")
            for k in range(8):
                if k < 3:
                    wo_ap.append(w0f[:, k * 1024:(k + 1) * 1024]); wo_buf.append(wslb[0])
                elif k < 6:
                    wo_ap.append(w1f[:, (k - 3) * 1024:(k - 2) * 1024]); wo_buf.append(wslb[1])
                else:
                    wo_ap.append(vT[:, (k - 6) * 1024:(k - 5) * 1024]); wo_buf.append(vTb)
            woc = [S.dma_ctr("a_woc%d" % k) for k in range(3)]
            for k in range(8):
                P.dma("pool", wo_ap[k], wo_src[:, k, :], woc[0 if k < 3 else (1 if k < 6 else 2)], [], [wo_buf[k]])

        pending_norm = []

        def emit_norm():
            while pending_norm:
                hp_ = pending_norm.pop(0)
                for ch in range(4):
                    cs = slice(ch * 512, (ch + 1) * 512)
                    P.v("dve", (lambda a: lambda e: e.reciprocal(a, a))(dacc[:, cs]), [daccb], [daccb])
                for ch in range(4):
                    cs = slice(ch * 512, (ch + 1) * 512)
                    P.v("pool", (lambda o, a, b: lambda e: e.tensor_tensor(o, a, b, ALU.mult))(oT[:, hp_, cs], nacc[:, cs], dacc[:, cs]),
                        [naccb, daccb], [oTb[hp_]])

        for hp in range(8):
            for gi, r in enumerate(GROUPS):
                L = S_LEN // r
                tpc = L // 128
                it = hp * 3 + gi
                sl = it % 2
                bs = it % 2
                if it == 0:
                    load_w(0)
                    load_bm(0)
                if it + 1 < 24:
                    load_bm(it + 1)
                emit_norm()
                lpt = 512 // r
                order = [(s3, tg) for s3 in range(3) for tg in range(4)]
                if it == 0:
                    order = [(s3, tg) for tg in range(4) for s3 in range(3)]
                for (s3, tg) in order:
                    if it == 0 and s3 == 0:
                        self.norm_pre_many(range(tg * 4, tg * 4 + 4), lambda t: (hT[:, :, t * 128:(t + 1) * 128], hTb[t]), tr_banks=(0, 7))
                        if tg == 0:
                            load_w(1)
                    if True:
                        bk = qkv_banks.next()
                        for kc in range(8):
                            P.mm(self.ps[bk], wsl[sl][:, kc, s3, :], hT[:, kc, tg * 512:(tg + 1) * 512], kc == 0, kc == 7,
                                 [wslb[sl]] + hTb[tg * 4:(tg + 1) * 4], [self.psb[bk]])
                        pin = self.ps[bk].rearrange("p (l c) -> p c l", c=r)
                        if s3 == 0:
                            for h in range(2):
                                rows = slice(64 * h, 64 * h + 64)
                                o = qZ[rows, h, :].rearrange("p (c l) -> p c l", c=r)[:, :, tg * lpt:(tg + 1) * lpt]
                                P.act(o, pin[rows], AF.Copy, [self.psb[bk]], [qZb], scale=0.125)
                        else:
                            dst, dstb = (kT, kTb) if s3 == 1 else (vT, vTb)
                            o = dst[:, :].rearrange("p (c l) -> p c l", c=r)[:, :, tg * lpt:(tg + 1) * lpt]
                            if gi == 0 and hp > 0:
                                P.act(o, pin, AF.Copy, [self.psb[bk]], [dstb])
                            else:
                                P.v("dve", (lambda o, pin: lambda e: e.tensor_copy(o, pin))(o, pin), [self.psb[bk]], [dstb])
                if it + 2 < 24:
                    load_w(it + 2)
                if it >= 1:
                    self.pump(2)
                for half in range(2):
                    bk = qkv_banks.next()
                    pv = self.ps[bk].bitcast(BF16)
                    for k8 in range(8):
                        ti = half * 8 + k8
                        P.tr(pv[:, k8 * 128:(k8 + 1) * 128], vT[:, ti * 128:(ti + 1) * 128], self.ident[:, :],
                             [vTb, self.identb], [self.psb[bk]])
                    pv3 = pv.rearrange("p (t c) -> p t c", t=8)
                    P.act(VZ[:, half * 8:(half + 1) * 8, 0, 0:64], pv3[:, :, 0:64], AF.Copy, [self.psb[bk]], [VZb])
                    P.v("dve", (lambda o, i_: lambda e: e.tensor_copy(o, i_))(VZ[:, half * 8:(half + 1) * 8, 1, 64:128], pv3[:, :, 64:128]),
                        [self.psb[bk]], [VZb])
                if it == 23:
                    load_wo()
                if r == 1:
                    units = [[(0, m * 512, (m + 1) * 512, 0)] for m in range(4)]
                elif r == 4:
                    units = [[(c, 0, 512, 0)] for c in range(4)]
                else:
                    units = [[(c0 + k, 0, 128, k * 128) for k in range(4)] for c0 in range(0, 16, 4)]
                steps = []
                for unit in units:
                    us = []
                    for (c, qlo, qhi, coff) in unit:
                        for j in range(tpc):
                            lo = max(128 * j - 64, qlo)
                            hi = min(128 * j + 192, qhi)
                            if hi > lo:
                                us.append([c, j, lo, hi, coff, qlo, False, False, unit])
                    us[0][6] = True
                    us[-1][7] = True
                    steps.extend(tuple(u) for u in us)
                prev = None
                ndset = None
                for stp in steps:
                    pk = stage_a(stp, r, L, tpc, bs)
                    if prev is not None:
                        stage_b(prev[0], prev[1], r, L, tpc, gi, prev[2])
                    if stp[6]:
                        ndset = nd_sets.next()
                    prev = (stp, pk, ndset)
                stage_b(prev[0], prev[1], r, L, tpc, gi, prev[2])
            pending_norm.append(hp)

        emit_norm()
        wo_banks = Rot([1, 3])
        self.resid_eng = "dve"
        for t in range(NT):
            b0 = wo_banks.next()
            for nh in range(2):
                for k in range(8):
                    P.mm(self.ps[b0 + nh], oT[:, k, t * 128:(t + 1) * 128], wo_ap[k][:, nh * 512:(nh + 1) * 512],
                         k == 0, k == 7, [oTb[k], wo_buf[k]], [self.psb[b0 + nh]])
            self.post_norm_residual(t, b0)


_CONSTS = None


def _t5_bucket(rel):
    rel = np.asarray(rel, dtype=np.int64)
    nb, max_exact = 16, 8
    base = np.where(rel > 0, nb, 0)
    n = np.abs(rel)
    nf = np.maximum(n, 1).astype(np.float32)
    lg = (np.log(nf / np.float32(max_exact)) / np.float32(np.log(1024 / max_exact)) * np.float32(nb - max_exact)).astype(np.float32)
    large = max_exact + lg.astype(np.int32)
    large = np.minimum(large, nb - 1)
    return base + np.where(n < max_exact, n, large)


def _consts():
    global _CONSTS
    if _CONSTS is None:
        oh = np.zeros((32, 3, 384), dtype=np.float32)
        mrow = np.full((1, 384), -30000.0, dtype=np.float32)
        for gi, r in enumerate((1, 4, 16)):
            for u in range(127, 256):
                delta = 191 - u
                oh[int(_t5_bucket(delta * r)), gi, u] = 1.0
        mrow[0, 127:256] = 0.0
        _CONSTS = {
            "c_ident": np.eye(128, dtype=np.float32).astype(ml_dtypes.bfloat16),
            "c_J": np.ascontiguousarray(np.eye(128, dtype=np.float32)[::-1]).astype(ml_dtypes.bfloat16),
            "c_oh": oh,
            "c_mrow": mrow,
        }
    return _CONSTS


_NC_CACHE = {}


def get_nc(phases):
    key = tuple(phases)
    if key not in _NC_CACHE:
        _NC_CACHE[key] = Prog(phases).build()
    return _NC_CACHE[key]


def run(inputs, phases, trace=False):
    nc = get_nc(phases)
    x = np.ascontiguousarray(inputs["x"], dtype=np.float32)
    shared = {k: np.ascontiguousarray(v) for k, v in inputs.items() if k != "x"}
    in_maps = []
    for c in range(8):
        m = {"x": x[c]}
        for k in ("norm_mix_pre", "norm_mix_post", "norm_mlp_pre", "norm_mlp_post", "mlp_w_up", "mlp_w_down"):
            m[k] = shared[k]
        if "conv" in phases:
            def colmaj(v):
                return np.ascontiguousarray(np.asarray(v, dtype=np.float32).reshape(-1, 128).T)
            m["conv_w_pw1"] = shared["conv_w_pw1"]
            m["conv_w_pw2"] = shared["conv_w_pw2"]
            m["conv_b_pw2"] = shared["conv_b_pw2"]
            m["l_conv_ln_g"] = colmaj(shared["conv_ln_g"][0])
            m["l_conv_ln_b"] = colmaj(shared["conv_ln_b"][0])
            m["l_conv_b_dw"] = colmaj(shared["conv_b_dw"][0])
            m["l_conv_b_pw1"] = colmaj(shared["conv_b_pw1"][0])
            m["l_conv_w_dw"] = np.ascontiguousarray(shared["conv_w_dw"][0].T.reshape(8, 128, 31))
        if "attn" in phases:
            m["attn_w_qkv"] = shared["attn_w_qkv"]
            m["attn_w_o"] = shared["attn_w_o"]
            m["rel_bias"] = shared["rel_bias"]
        cs = _consts()
        m["c_ident"] = cs["c_ident"]
        if "attn" in phases:
            for k in ("c_J", "c_oh", "c_mrow"):
                m[k] = cs[k]
        in_maps.append(m)
    res = run_bass_kernel_spmd(nc, in_maps, core_ids=list(range(8)), trace=trace)
    out = np.stack([np.asarray(r["y"]) for r in res.results], axis=0).astype(np.float32)
    return out, res


def kernel(**inputs):
    out, _ = run(inputs, ("attn", "mlp0", "conv", "mlp1"))
    return out
```
